# Optimizing a Trainium2 kernel written in Bass

```python
import jax
import jax.numpy as jnp
from jax import lax

D_MODEL = 1024
BATCH = 8
SEQ = 4096
DEPTH = 2

D_RNN = D_MODEL
D_POOL = D_MODEL
D_MIX = D_RNN + D_POOL
N_RNN_HEADS = 8
RNN_HEAD_DIM = D_RNN // N_RNN_HEADS
CONV_WIDTH = 4
LRU_C = 8.0
POOL_WINDOWS = (2, 4, 8, 16)
N_POOL_GROUPS = len(POOL_WINDOWS)
POOL_GROUP_DIM = D_POOL // N_POOL_GROUPS
NORM_EPS = 1e-6

kernel_name = "hybrid_rglru_multiscale_pool_parallel_heads"


def rmsnorm(x, g):
    xf = x.astype(jnp.float32)
    y = xf * lax.rsqrt(jnp.mean(xf * xf, axis=-1, keepdims=True) + NORM_EPS)
    return (y * g.astype(jnp.float32)).astype(x.dtype)


def causal_depthwise_conv(x, w, b):
    y = lax.conv_general_dilated(
        x, w[:, None, :].astype(x.dtype), window_strides=(1,),
        padding=[(CONV_WIDTH - 1, 0)],
        dimension_numbers=("NWC", "WIO", "NWC"),
        feature_group_count=x.shape[-1])
    return y + b


def rg_lru(x, w_a, b_a, w_x, b_x, lam):
    B, S, _ = x.shape
    xh = x.reshape(B, S, N_RNN_HEADS, RNN_HEAD_DIM)
    r = jax.nn.sigmoid(jnp.einsum("bshi,hij->bshj", xh, w_a) + b_a).reshape(B, S, D_RNN)
    i = jax.nn.sigmoid(jnp.einsum("bshi,hij->bshj", xh, w_x) + b_x).reshape(B, S, D_RNN)
    log_a = -LRU_C * r.astype(jnp.float32) * jax.nn.softplus(-lam.astype(jnp.float32))
    a = jnp.exp(log_a)
    mult = jnp.sqrt(-jnp.expm1(2.0 * log_a))
    u = mult * (i * x).astype(jnp.float32)

    def step(h, inp):
        a_t, u_t = inp
        h = a_t * h + u_t
        return h, h

    h0 = jnp.zeros((B, D_RNN), jnp.float32)
    _, hs = lax.scan(step, h0, (jnp.swapaxes(a, 0, 1), jnp.swapaxes(u, 0, 1)))
    return jnp.swapaxes(hs, 0, 1).astype(x.dtype)


def multi_scale_pool(x, w, b, scale):
    B, S, _ = x.shape
    xg = x.reshape(B, S, N_POOL_GROUPS, POOL_GROUP_DIM).astype(jnp.float32)
    cs = jnp.cumsum(xg, axis=1)
    t = jnp.arange(S)
    means = []
    for g, win in enumerate(POOL_WINDOWS):
        csg = cs[:, :, g]
        lagged = jnp.pad(csg, ((0, 0), (win, 0), (0, 0)))[:, :S]
        count = jnp.minimum(t + 1, win).astype(jnp.float32)[None, :, None]
        means.append((csg - lagged) / count)
    pooled = (jnp.stack(means, axis=2) - xg).astype(x.dtype)
    y = jnp.einsum("bsgi,gij->bsgj", pooled, w) + b
    return y.reshape(B, S, D_POOL) * scale


def setup_inputs(seed: int = 0) -> dict:
    key = jax.random.key(seed)
    ks = jax.random.split(key, 24)
    f32 = jnp.float32
    nrm = lambda k, shape, s: jax.random.normal(k, shape, f32) * s
    L = DEPTH
    x = jax.random.normal(ks[0], (BATCH, SEQ, D_MODEL), f32)
    c = jax.random.normal(ks[1], (BATCH, D_MODEL), f32)
    ada_w = nrm(ks[2], (L, D_MODEL, 3 * D_MODEL), 0.5 * D_MODEL ** -0.5)
    ada_b = nrm(ks[3], (L, 3 * D_MODEL), 0.01)
    pre_norm_g = 1.0 + nrm(ks[4], (L, D_MODEL), 0.05)
    w_in = nrm(ks[5], (L, D_MODEL, 2 * D_MIX), D_MODEL ** -0.5)
    conv_w = nrm(ks[6], (L, CONV_WIDTH, D_RNN), CONV_WIDTH ** -0.5)
    conv_b = nrm(ks[7], (L, D_RNN), 0.01)
    gate_a_w = nrm(ks[8], (L, N_RNN_HEADS, RNN_HEAD_DIM, RNN_HEAD_DIM), RNN_HEAD_DIM ** -0.5)
    gate_a_b = nrm(ks[9], (L, N_RNN_HEADS, RNN_HEAD_DIM), 0.01)
    gate_x_w = nrm(ks[10], (L, N_RNN_HEADS, RNN_HEAD_DIM, RNN_HEAD_DIM), RNN_HEAD_DIM ** -0.5)
    gate_x_b = nrm(ks[11], (L, N_RNN_HEADS, RNN_HEAD_DIM), 0.01)
    a_c = jax.random.uniform(ks[12], (L, D_RNN), f32, 0.9, 0.999)
    a0 = a_c ** (1.0 / LRU_C)
    lru_lambda = jnp.log(a0) - jnp.log1p(-a0)
    pool_w = nrm(ks[13], (L, N_POOL_GROUPS, POOL_GROUP_DIM, POOL_GROUP_DIM), POOL_GROUP_DIM ** -0.5)
    pool_b = nrm(ks[14], (L, N_POOL_GROUPS, POOL_GROUP_DIM), 0.01)
    pool_scale = jax.random.uniform(ks[15], (L, D_POOL), f32, 0.5, 1.5)
    w_out = nrm(ks[16], (L, D_MIX, D_MODEL), D_MIX ** -0.5)
    post_norm_g = 1.0 + nrm(ks[17], (L, D_MODEL), 0.05)
    return {"x": x, "c": c, "ada_w": ada_w, "ada_b": ada_b, "pre_norm_g": pre_norm_g,
            "w_in": w_in, "conv_w": conv_w, "conv_b": conv_b,
            "gate_a_w": gate_a_w, "gate_a_b": gate_a_b, "gate_x_w": gate_x_w, "gate_x_b": gate_x_b,
            "lru_lambda": lru_lambda, "pool_w": pool_w, "pool_b": pool_b, "pool_scale": pool_scale,
            "w_out": w_out, "post_norm_g": post_norm_g}


def reference(x, c, ada_w, ada_b, pre_norm_g, w_in, conv_w, conv_b,
              gate_a_w, gate_a_b, gate_x_w, gate_x_b, lru_lambda,
              pool_w, pool_b, pool_scale, w_out, post_norm_g):
    c_act = jax.nn.silu(c)
    for l in range(DEPTH):
        mod = c_act @ ada_w[l] + ada_b[l]
        shift, scale, gate = jnp.split(mod, 3, axis=-1)
        h = rmsnorm(x, pre_norm_g[l]) * (1.0 + scale[:, None, :]) + shift[:, None, :]
        proj = h @ w_in[l]
        x_rnn, g_rnn, x_pool, g_pool = jnp.split(
            proj, [D_RNN, 2 * D_RNN, 2 * D_RNN + D_POOL], axis=-1)
        u = causal_depthwise_conv(x_rnn, conv_w[l], conv_b[l])
        y_rnn = rg_lru(u, gate_a_w[l], gate_a_b[l], gate_x_w[l], gate_x_b[l],
                       lru_lambda[l]) * jax.nn.silu(g_rnn)
        y_pool = multi_scale_pool(x_pool, pool_w[l], pool_b[l], pool_scale[l]) * jax.nn.silu(g_pool)
        y = jnp.concatenate([y_rnn, y_pool], axis=-1) @ w_out[l]
        x = x + gate[:, None, :] * rmsnorm(y, post_norm_g[l])
    return x
```

```python
import numpy as np
import concourse.bass as bass
import concourse.mybir as mybir
from concourse.alu_op_type import AluOpType as ALU
from concourse.bass_utils import run_bass_kernel_spmd

F32 = mybir.dt.float32
BF16 = mybir.dt.bfloat16
AF = mybir.ActivationFunctionType

ENGS = ("pe", "act", "dve", "pool", "sp")


class _Op:
    __slots__ = ("eng", "fn", "reads", "writes", "deps", "signal", "sem", "val", "dkey", "ndma", "banks")

    def __init__(self, eng, fn, reads, writes, dkey, ndma, banks=()):
        self.eng, self.fn, self.reads, self.writes = eng, fn, tuple(reads), tuple(writes)
        self.banks = tuple(banks)
        self.deps = []
        self.signal = False
        self.sem = None
        self.val = 0
        self.dkey = dkey
        self.ndma = ndma


class Sched:
    def __init__(self, same_eng_wait=True):
        self.ops = []
        self.same_eng_wait = same_eng_wait

    def add(self, eng, fn, reads=(), writes=(), dkey=None, ndma=1, banks=()):
        assert eng in ENGS
        if eng == "sp" and dkey is None:
            dkey = writes[0] if writes else reads[0]
        self.ops.append(_Op(eng, fn, reads, writes, dkey, ndma, banks))

    def analyze(self):
        last_w = {}
        readers = {}
        bank_last = {}
        pos_in_eng = {}
        cnt = {e: 0 for e in ENGS}
        for i, op in enumerate(self.ops):
            pos_in_eng[i] = cnt[op.eng]
            cnt[op.eng] += 1
            deps = set()
            for r in op.reads:
                if r in last_w:
                    deps.add(last_w[r])
            for w in op.writes:
                if w in last_w:
                    deps.add(last_w[w])
                for j in readers.get(w, ()):
                    deps.add(j)
            for b in op.banks:
                bl = bank_last.setdefault(b, {})
                for e2, j in bl.items():
                    if e2 != op.eng:
                        deps.add(j)
                bl[op.eng] = i
            deps.discard(i)
            best = {}
            keep = []
            for j in deps:
                e = self.ops[j].eng
                if e == "sp":
                    keep.append(j)
                    continue
                if e == op.eng:
                    if e == "pe":
                        continue
                    if not self.same_eng_wait:
                        continue
                if e not in best or j > best[e]:
                    best[e] = j
            keep.extend(best.values())
            op.deps = sorted(keep)
            for j in op.deps:
                self.ops[j].signal = True
            for w in op.writes:
                last_w[w] = i
                readers[w] = []
            for r in op.reads:
                readers.setdefault(r, []).append(i)

    def emit(self, nc, block, sems, dma_sems, final_wait_keys=()):
        self.analyze()
        cnt = {e: 0 for e in ENGS}
        dcnt = {}
        for op in self.ops:
            if op.eng == "sp":
                dcnt[op.dkey] = dcnt.get(op.dkey, 0) + 16 * op.ndma
                op.sem, op.val = dma_sems[op.dkey], dcnt[op.dkey]
            elif op.signal:
                cnt[op.eng] += 1
                op.sem, op.val = sems[op.eng], cnt[op.eng]
        ops = self.ops

        def run(eng_name, e):
            waited = {}
            for op in ops:
                if op.eng != eng_name:
                    continue
                for j in op.deps:
                    d = ops[j]
                    key = id(d.sem)
                    if waited.get(key, 0) >= d.val:
                        continue
                    e.wait_ge(d.sem, d.val)
                    waited[key] = d.val
                res = op.fn(e)
                if op.eng == "sp":
                    lst = res if isinstance(res, (list, tuple)) else [res]
                    assert len(lst) == op.ndma, (len(lst), op.ndma)
                    for ins in lst:
                        ins.then_inc(op.sem, 16)
                elif op.signal:
                    ins = res[-1] if isinstance(res, (list, tuple)) else res
                    ins.then_inc(op.sem, 1)
            if eng_name == "sp":
                for k in final_wait_keys:
                    if k in dcnt:
                        e.wait_ge(dma_sems[k], dcnt[k])

        @block.tensor
        def _(e):
            run("pe", e)

        @block.scalar
        def _(e):
            run("act", e)

        @block.vector
        def _(e):
            run("dve", e)

        @block.gpsimd
        def _(e):
            run("pool", e)

        @block.sync
        def _(e):
            run("sp", e)


D = 1024
SEQ = 4096
NB = 8
DEPTH = 2
T = 256
NS = T // 128
NT = SEQ // T
EPS = 1e-6
KC = 8
SAME_ENG_WAIT = True
NV = 88
LOG1P_COEF = [1.0 / (2 * k + 1) for k in range(8)]


def build_nc(layers, seq=SEQ):
    from contextlib import ExitStack
    nt = seq // T
    nc = bass.Bass("TRN2", target_bir_lowering=False)
    NL = len(layers)

    def din(name, shape, dt=F32):
        return nc.dram_tensor(name, shape, dt, kind="ExternalInput").ap()

    x_in = din("x", [seq, D])
    cbc_d = din("cbc", [128, KC * 128])
    ada_w_d = din("ada_w", [DEPTH * 6 * 2, 128, 2048])
    ada_b_d = din("ada_b_bc", [DEPTH, 128, 3 * D])
    postg_d = din("post_g_bc", [DEPTH, 128, D])
    w_in_d = din("w_in", [DEPTH * 8 * 2, 128, 2048])
    w_out_d = din("w_out", [DEPTH * 4 * 2, 128, 2048])
    ga_w_d = din("ga_w", [DEPTH, 128, 1024])
    gx_w_d = din("gx_w", [DEPTH, 128, 1024])
    pw_d = din("pool_w", [DEPTH, 128, 2048])
    vecs_d = din("vecs", [DEPTH, 128, NV])
    ident_d = din("ident", [128, 128])
    invc_d = din("invcnt", [128, 8 * 16])
    out_d = nc.dram_tensor("out", [seq, D], F32, kind="ExternalOutput").ap()
    mids = [nc.dram_tensor(f"xmid{i}", [seq, D], F32, kind="Internal").ap() for i in range(NL - 1)]

    with ExitStack() as es:
        def sb(name, shape, dt=F32):
            return es.enter_context(nc.sbuf_tensor(name, shape, dt))

        def ps(name, shape, dt=F32):
            return es.enter_context(nc.psum_tensor(name, shape, dt))

        w_in_bf = sb("w_in_bf", [128, KC * 4096], BF16)
        w_out_bf = sb("w_out_bf", [128, 16 * 1024], BF16)
        ga_bf = sb("ga_bf", [128, 1024], BF16)
        gx_bf = sb("gx_bf", [128, 1024], BF16)
        pw_bf = sb("pw_bf", [128, 2048], BF16)
        U = sb("U", [128, 8192], F32)
        xs = [sb(f"xs{i}", [128, D]) for i in range(2)]
        xn = [sb(f"xn{i}", [128, D], BF16) for i in range(2)]
        hT = [sb(f"hT{i}", [128, KC * T], BF16) for i in range(2)]
        xr = [sb(f"xr{i}", [128, 4 + T]) for i in range(2)]
        ub = [sb(f"u{i}", [128, T]) for i in range(2)]
        ubf = [sb(f"ubf{i}", [128, T], BF16) for i in range(2)]
        hb = [sb(f"h{i}", [128, T]) for i in range(2)]
        xp = [sb(f"xp{i}", [128, 16 + T]) for i in range(2)]
        tmpA = sb("tmpA", [128, 16 + T])
        tmpB = sb("tmpB", [128, 16 + T])
        pl = [sb(f"pl{i}", [128, 2 * T], BF16) for i in range(2)]
        sgp = [sb(f"sgp{i}", [128, T]) for i in range(4)]
        yT = sb("yT", [128, 16 * T], BF16)
        xo = [sb(f"xo{i}", [128, D]) for i in range(2)]
        tmpo = sb("tmpo", [128, D])
        gg = sb("gg", [128, D])
        vecs = sb("vecs_sb", [128, NV])
        identf = sb("identf", [128, 128])
        identb = sb("identb", [128, 128], BF16)
        invc = sb("invc", [128, 128])
        mhalf = sb("mhalf", [128, 8])
        small = sb("small", [128, 256])
        stat = sb("stat", [128, 64])
        hx = sb("hx", [128, 8 * 4])
        hp = sb("hp", [128, 8 * 16])
        hstate = sb("hstate", [128, 8])

        def Uv(off, n):
            return U[:, off:off + n]
        A = [Uv(c * T, T) for c in range(8)]
        A2 = [Uv(2048 + c * T, T) for c in range(8)]
        T1 = [Uv(4096 + c * T, T) for c in range(8)]
        SG = [Uv(6144 + c * T, T) for c in range(8)]
        stage = [Uv(0, 2048), Uv(2048, 2048)]
        mod_bc = Uv(4096, 3072)
        cact = Uv(7168, 1024)

        SM = {}
        _o = [0]

        def smalloc(name, n):
            SM[name] = (_o[0], n)
            _o[0] += n
        for nm in ("shift", "scale", "gs", "nsp4", "nsp8", "hga", "hgx"):
            smalloc(nm, 8)
        smalloc("ws", 16)
        smalloc("pb", 32)
        smalloc("hpb", 32)
        for nm in ("sp_e", "sp_z", "sp_z2", "sp_p", "sp_t", "sp_ax", "sp_mx"):
            smalloc(nm, 8)
        assert _o[0] <= 256

        def sm(name, c=None, n=1):
            o, w = SM[name]
            if c is None:
                return small[:, o:o + w]
            return small[:, o + c:o + c + n]

        def vcol(o, c=None, n=8):
            if c is None:
                return vecs[:, o:o + n]
            return vecs[:, o + c:o + c + 1]

        pj = [ps(f"pj{i}", [128, 2 * T]) for i in range(2)]
        pz = [ps(f"pz{i}", [128, 2 * T]) for i in range(2)]
        pq = ps("pq", [128, 2 * T])
        ptr = ps("ptr", [128, KC * 128], BF16)
        po = ps("po", [128, D])

        sems = {e: es.enter_context(nc.semaphore("s_" + e)) for e in ENGS if e != "sp"}
        dkeys = ["xs0", "xs1", "xo0", "xo1", "stage0", "stage1", "identf", "invc", "cact", "vecs", "gg"] + [f"mod_bc{i}" for i in range(6)]
        dsems = {k: es.enter_context(nc.semaphore("d_" + k)) for k in dkeys}
        block = es.enter_context(nc.Block())
        S = Sched(same_eng_wait=SAME_ENG_WAIT)
        add = S.add

        def ts1(e, out, in0, s1, op0=ALU.mult):
            if op0 == ALU.mult:
                return e.tensor_scalar(out=out, in0=in0, scalar1=s1, scalar2=0.0, op0=ALU.mult, op1=ALU.add)
            assert op0 == ALU.add
            return e.tensor_scalar(out=out, in0=in0, scalar1=s1, scalar2=1.0, op0=ALU.add, op1=ALU.mult)

        add("sp", lambda e: e.dma_start(out=identf[:], in_=ident_d), writes=["identf"])
        add("sp", lambda e: e.dma_start(out=invc[:], in_=invc_d), writes=["invc"])
        add("dve", lambda e: e.tensor_copy(out=identb[:], in_=identf[:]), reads=["identf"], writes=["identb"])
        add("pool", lambda e: e.memset(mhalf[:], -0.5), writes=["mhalf"])
        cnt = {"stage": 0, "cast": 0}

        def barrier(keys, token):
            add("pool", lambda e: e.memset(stat[:, 63:64], 0.0), writes=list(keys) + [token, "stat63"])

        U_MAIN_KEYS = [f"A{c}" for c in range(8)] + [f"A2{c}" for c in range(8)] + \
                      [f"T1{c}" for c in range(8)] + [f"SG{c}" for c in range(8)]
        MODK = [f"mod_bc{i}" for i in range(6)]
        U_PRO_KEYS = ["stage0", "stage1", "cact"] + MODK

        def stage_load(src_ap, width=2048):
            i = cnt["stage"] % 2
            cnt["stage"] += 1
            add("sp", lambda e: e.dma_start(out=stage[i][:, 0:width], in_=src_ap), reads=["U_main_done"], writes=[f"stage{i}"], dkey=f"stage{i}")
            return i

        def scaled_cast(dst, src, scal, skey, reads, writes):
            cnt["cast"] += 1
            eng = ("act", "dve", "pool")[cnt["cast"] % 3]
            if eng == "act":
                add("act", lambda e: e.activation(out=dst, in_=src, func=AF.Copy, scale=scal), reads=reads + [skey], writes=writes)
            else:
                add(eng, lambda e: ts1(e, dst, src, scal), reads=reads + [skey], writes=writes)

        def prologue(l):
            add("sp", lambda e: e.dma_start(out=cact, in_=cbc_d), reads=["U_main_done"], writes=["cact"])
            add("act", lambda e: e.activation(out=mod_bc[:, 0:1024], in_=cact, func=AF.Tanh, scale=0.5), reads=["cact", "U_main_done"], writes=["mod_bc0", "mod_bc1"])
            add("dve", lambda e: e.scalar_tensor_tensor(out=cact, in0=mod_bc[:, 0:1024], scalar=1.0, in1=cact, op0=ALU.add, op1=ALU.mult),
                reads=["mod_bc0", "mod_bc1", "cact"], writes=["cact"])
            add("dve", lambda e: ts1(e, cact, cact, 0.5), reads=["cact"], writes=["cact"])
            add("sp", lambda e: e.dma_start(out=vecs[:], in_=vecs_d[l]), writes=["vecs"])
            for nb in range(6):
                for kh in range(2):
                    si = stage_load(ada_w_d[(l * 6 + nb) * 2 + kh])

                    def mm(e, si=si, kh=kh):
                        r = []
                        for k4 in range(4):
                            kc = kh * 4 + k4
                            r.append(e.matmul(po[:, 0:512], lhsT=cact[:, kc * 128:(kc + 1) * 128], rhs=stage[si][:, k4 * 512:(k4 + 1) * 512],
                                              start=(kc == 0), stop=(kc == 7)))
                        return r
                    add("pe", mm, reads=[f"stage{si}", "cact"], writes=["po"], banks=["PO0"])
                add("sp", lambda e, nb=nb: e.dma_start(out=mod_bc[:, nb * 512:(nb + 1) * 512], in_=ada_b_d[l][:, nb * 512:(nb + 1) * 512]),
                    reads=["U_main_done"], writes=[f"mod_bc{nb}"])
                add("dve", lambda e, nb=nb: e.tensor_tensor(out=mod_bc[:, nb * 512:(nb + 1) * 512], in0=po[:, 0:512], in1=mod_bc[:, nb * 512:(nb + 1) * 512], op=ALU.add),
                    reads=["po", f"mod_bc{nb}"], writes=[f"mod_bc{nb}"], banks=["PO0"])
            for which, base in (("shift", 0), ("scale", 1024)):
                for kc in range(8):
                    nbk = (base + kc * 128) // 512
                    add("dve", lambda e, which=which, base=base, kc=kc: e.scalar_tensor_tensor(
                        out=tmpA[:, 0:128], in0=mod_bc[:, base + kc * 128:base + (kc + 1) * 128], scalar=1.0, in1=identf[:],
                        op0=ALU.mult, op1=ALU.mult, accum_out=sm(which, kc)),
                        reads=[f"mod_bc{nbk}", "identf"], writes=["tmpA", f"sm_{which}"])
            add("dve", lambda e: e.scalar_tensor_tensor(out=sm("gs"), in0=sm("scale"), scalar=1.0, in1=vcol(0), op0=ALU.add, op1=ALU.mult),
                reads=["sm_scale", "vecs"], writes=["sm_gs"])
            add("sp", lambda e: e.dma_start(out=gg[:], in_=postg_d[l]), writes=["gg"])
            for hh in range(2):
                add("pool", lambda e, hh=hh: e.tensor_tensor(out=gg[:, hh * 512:(hh + 1) * 512], in0=gg[:, hh * 512:(hh + 1) * 512],
                                                            in1=mod_bc[:, 2048 + hh * 512:2048 + (hh + 1) * 512], op=ALU.mult),
                    reads=["gg", f"mod_bc{4 + hh}"], writes=["gg"])
            lam = vcol(64)
            add("dve", lambda e: e.scalar_tensor_tensor(out=sm("sp_ax"), in0=lam, scalar=-1.0, in1=lam, op0=ALU.mult, op1=ALU.max), reads=["vecs"], writes=["sm_sp_ax"])
            add("act", lambda e: e.activation(out=sm("sp_e"), in_=sm("sp_ax"), func=AF.Exp, scale=-1.0), reads=["sm_sp_ax"], writes=["sm_sp_e"])
            add("dve", lambda e: ts1(e, sm("sp_t"), sm("sp_e"), 2.0, ALU.add), reads=["sm_sp_e"], writes=["sm_sp_t"])
            add("dve", lambda e: e.reciprocal(out=sm("sp_t"), in_=sm("sp_t")), reads=["sm_sp_t"], writes=["sm_sp_t"])
            add("dve", lambda e: e.tensor_tensor(out=sm("sp_z"), in0=sm("sp_e"), in1=sm("sp_t"), op=ALU.mult), reads=["sm_sp_e", "sm_sp_t"], writes=["sm_sp_z"])
            add("dve", lambda e: e.tensor_tensor(out=sm("sp_z2"), in0=sm("sp_z"), in1=sm("sp_z"), op=ALU.mult), reads=["sm_sp_z"], writes=["sm_sp_z2"])
            add("dve", lambda e: e.tensor_scalar(out=sm("sp_p"), in0=sm("sp_z2"), scalar1=LOG1P_COEF[7], scalar2=LOG1P_COEF[6], op0=ALU.mult, op1=ALU.add),
                reads=["sm_sp_z2"], writes=["sm_sp_p"])
            for k in (5, 4, 3, 2, 1, 0):
                add("dve", lambda e: e.tensor_tensor(out=sm("sp_p"), in0=sm("sp_p"), in1=sm("sp_z2"), op=ALU.mult), reads=["sm_sp_p", "sm_sp_z2"], writes=["sm_sp_p"])
                add("dve", lambda e, k=k: ts1(e, sm("sp_p"), sm("sp_p"), LOG1P_COEF[k], ALU.add), reads=["sm_sp_p"], writes=["sm_sp_p"])
            add("dve", lambda e: e.tensor_tensor(out=sm("sp_p"), in0=sm("sp_p"), in1=sm("sp_z"), op=ALU.mult), reads=["sm_sp_p", "sm_sp_z"], writes=["sm_sp_p"])
            add("dve", lambda e: e.tensor_scalar(out=sm("sp_mx"), in0=lam, scalar1=-1.0, scalar2=0.0, op0=ALU.mult, op1=ALU.max), reads=["vecs"], writes=["sm_sp_mx"])
            add("dve", lambda e: e.scalar_tensor_tensor(out=sm("sp_t"), in0=sm("sp_p"), scalar=2.0, in1=sm("sp_mx"), op0=ALU.mult, op1=ALU.add),
                reads=["sm_sp_p", "sm_sp_mx", "sm_sp_t"], writes=["sm_sp_t"])
            add("dve", lambda e: ts1(e, sm("nsp4"), sm("sp_t"), -4.0), reads=["sm_sp_t"], writes=["sm_nsp4"])
            add("dve", lambda e: ts1(e, sm("nsp8"), sm("sp_t"), -8.0), reads=["sm_sp_t"], writes=["sm_nsp8"])
            add("dve", lambda e: ts1(e, sm("hga"), vcol(48), 0.5), reads=["vecs"], writes=["sm_hga"])
            add("dve", lambda e: ts1(e, sm("hgx"), vcol(56), 0.5), reads=["vecs"], writes=["sm_hgx"])
            add("pool", lambda e: e.memset(sm("ws", 0, 8), 0.25), writes=["sm_ws"])
            add("dve", lambda e: ts1(e, sm("ws", 8, 8), vcol(80), 0.5), reads=["vecs", "sm_ws"], writes=["sm_ws"])
            for cb in range(8):
                for kh in range(2):
                    si = stage_load(w_in_d[(l * 8 + cb) * 2 + kh])
                    for k4 in range(4):
                        kc = kh * 4 + k4
                        dst = w_in_bf[:, kc * 4096 + cb * 512: kc * 4096 + (cb + 1) * 512]
                        src = stage[si][:, k4 * 512:(k4 + 1) * 512]
                        scaled_cast(dst, src, sm("gs", kc), "sm_gs", [f"stage{si}"], [f"w_in_bf{cb}"])

                    def mmb(e, si=si, kh=kh, cb=cb):
                        r = []
                        for jb in range(4):
                            col = 512 + kh * 32 + cb * 4 + jb
                            for k4 in range(4):
                                kc = kh * 4 + k4
                                r.append(e.matmul(po[:, col:col + 1],
                                                  lhsT=stage[si][:, k4 * 512 + jb * 128: k4 * 512 + (jb + 1) * 128],
                                                  rhs=sm("shift", kc), start=(k4 == 0), stop=(k4 == 3)))
                        return r
                    add("pe", mmb, reads=[f"stage{si}", "sm_shift"], writes=["po_b"], banks=["PO1"])
            add("dve", lambda e: e.tensor_copy(out=sm("hpb"), in_=po[:, 512:544]), reads=["po_b", "po"], writes=["sm_hpb"], banks=["PO1"])
            add("dve", lambda e: e.tensor_tensor(out=sm("pb"), in0=po[:, 544:576], in1=sm("hpb"), op=ALU.add), reads=["po_b", "po", "sm_hpb"], writes=["sm_pb"], banks=["PO1"])
            add("dve", lambda e: ts1(e, sm("hpb"), sm("pb"), 0.5), reads=["sm_pb", "sm_hpb"], writes=["sm_hpb"])
            for rb in range(4):
                for hh in range(2):
                    si = stage_load(w_out_d[(l * 4 + rb) * 2 + hh])
                    for c2 in range(2):
                        cc = rb * 4 + hh * 2 + c2
                        scaled_cast(w_out_bf[:, cc * 1024:(cc + 1) * 1024], stage[si][:, c2 * 1024:(c2 + 1) * 1024], sm("ws", cc), "sm_ws",
                                    [f"stage{si}"], ["w_out_bf"])
            si = stage_load(ga_w_d[l], 1024)
            add("pool", lambda e, si=si: e.tensor_copy(out=ga_bf[:], in_=stage[si][:, 0:1024]), reads=[f"stage{si}"], writes=["ga_bf"])
            si = stage_load(gx_w_d[l], 1024)
            add("pool", lambda e, si=si: e.tensor_copy(out=gx_bf[:], in_=stage[si][:, 0:1024]), reads=[f"stage{si}"], writes=["gx_bf"])
            si = stage_load(pw_d[l], 2048)
            add("dve", lambda e, si=si: e.tensor_copy(out=pw_bf[:], in_=stage[si][:, 0:2048]), reads=[f"stage{si}"], writes=["pw_bf"])
            barrier(U_PRO_KEYS, "U_pro_done")

        def stage_a(src, stag, t):
            hs = t % 2
            for s in range(NS):
                g = t * NS + s
                sl = g % 2
                row0 = t * T + s * 128
                add("sp", lambda e, sl=sl, row0=row0: e.dma_start(out=xs[sl][:], in_=src[row0:row0 + 128, :]), reads=[f"dram_{stag}_{g}"], writes=[f"xs{sl}"], dkey=f"xs{sl}")
                add("act", lambda e, sl=sl: e.activation(out=xn[sl][:], in_=xs[sl][:], func=AF.Square, accum_out=stat[:, sl:sl + 1]),
                    reads=[f"xs{sl}"], writes=[f"xn{sl}", f"ssA{sl}"])
                add("pool", lambda e, sl=sl: e.tensor_scalar(out=stat[:, 2 + sl:3 + sl], in0=stat[:, sl:sl + 1], scalar1=1.0 / D, scalar2=EPS, op0=ALU.mult, op1=ALU.add),
                    reads=[f"ssA{sl}"], writes=[f"vA{sl}"])
                add("pool", lambda e, sl=sl: e.tensor_tensor(out=stat[:, 4 + sl:5 + sl], in0=stat[:, 2 + sl:3 + sl], in1=mhalf[:, 0:1], op=ALU.pow),
                    reads=[f"vA{sl}", "mhalf"], writes=[f"rsA{sl}"])
                add("act", lambda e, sl=sl: e.activation(out=xn[sl][:], in_=xs[sl][:], func=AF.Copy, scale=stat[:, 4 + sl:5 + sl]),
                    reads=[f"xs{sl}", f"rsA{sl}"], writes=[f"xn{sl}"])

                def tr(e, sl=sl):
                    return [e.transpose(out=ptr[:, kc * 128:(kc + 1) * 128], in_=xn[sl][:, kc * 128:(kc + 1) * 128], identity=identb[:]) for kc in range(KC)]
                add("pe", tr, reads=[f"xn{sl}", "identb"], writes=["ptr"], banks=["PTR"])
                add("dve", lambda e, hs=hs, s=s: e.tensor_copy(
                    out=hT[hs][:, :].rearrange("p (k t) -> p k t", k=KC)[:, :, s * 128:(s + 1) * 128],
                    in_=ptr[:, :].rearrange("p (k t) -> p k t", k=KC)),
                    reads=["ptr"], writes=[f"hT{hs}_{s}"], banks=["PTR"])

        def w_in_pair(t, i, q1, q2):
            hs = t % 2

            def mm(e, i=i, q1=q1, q2=q2, hs=hs):
                r = []
                for half, q in ((0, q1), (1, q2)):
                    for kc in range(KC):
                        r.append(e.matmul(pj[i][:, half * T:(half + 1) * T], lhsT=w_in_bf[:, kc * 4096 + q * 128: kc * 4096 + (q + 1) * 128],
                                          rhs=hT[hs][:, kc * T:(kc + 1) * T], start=(kc == 0), stop=(kc == KC - 1)))
                return r
            add("pe", mm, reads=[f"w_in_bf{q1 // 4}", f"w_in_bf{q2 // 4}"] + [f"hT{hs}_{s}" for s in range(NS)], writes=[f"pj{i}a", f"pj{i}b"], banks=[f"PJ{i}"])

        def rnn_front(t, c):
            i = c % 2
            w_in_pair(t, i, c, 8 + c)
            pa, pb_ = pj[i][:, 0:T], pj[i][:, T:2 * T]
            if t == 0:
                add("pool", lambda e, i=i: e.memset(xr[i][:, 0:3], 0.0), writes=[f"xr{i}"])
            else:
                add("pool", lambda e, i=i, c=c: e.tensor_copy(out=xr[i][:, 0:3], in_=hx[:, c * 4:c * 4 + 3]), reads=[f"hx{c}"], writes=[f"xr{i}"])
            add("act", lambda e, i=i, pa=pa, c=c: e.activation(out=xr[i][:, 3:3 + T], in_=pa, func=AF.Identity, bias=sm("pb", c), scale=1.0),
                reads=[f"pj{i}a", "sm_pb"], writes=[f"xr{i}"], banks=[f"PJ{i}"])
            add("pool", lambda e, i=i, c=c: e.tensor_copy(out=hx[:, c * 4:c * 4 + 3], in_=xr[i][:, T:T + 3]), reads=[f"xr{i}"], writes=[f"hx{c}"])
            add("act", lambda e, pb_=pb_, c=c: e.activation(out=SG[c], in_=pb_, func=AF.Tanh, bias=sm("hpb", 8 + c), scale=0.5),
                reads=[f"pj{i}b", "sm_hpb", "U_pro_done"], writes=[f"SG{c}"], banks=[f"PJ{i}"])
            add("pool", lambda e, c=c: ts1(e, SG[c], SG[c], 1.0, ALU.add), reads=[f"SG{c}"], writes=[f"SG{c}"])
            add("dve", lambda e, i=i, c=c: e.tensor_scalar(out=ub[i][:], in0=xr[i][:, 0:T], scalar1=vcol(8, c * 4 + 0), scalar2=vcol(40, c), op0=ALU.mult, op1=ALU.add),
                reads=[f"xr{i}", "vecs"], writes=[f"u{i}"])
            for k in (1, 2, 3):
                add("dve", lambda e, i=i, c=c, k=k: e.scalar_tensor_tensor(out=ub[i][:], in0=xr[i][:, k:k + T], scalar=vcol(8, c * 4 + k), in1=ub[i][:],
                                                                           op0=ALU.mult, op1=ALU.add),
                    reads=[f"xr{i}", "vecs", f"u{i}"], writes=[f"u{i}"])
            add("dve", lambda e, pb_=pb_, c=c: e.scalar_tensor_tensor(out=SG[c], in0=pb_, scalar=sm("pb", 8 + c), in1=SG[c], op0=ALU.add, op1=ALU.mult),
                reads=[f"pj{i}b", "sm_pb", f"SG{c}"], writes=[f"SG{c}"], banks=[f"PJ{i}"])
            add("pool", lambda e, i=i: e.tensor_copy(out=ubf[i][:], in_=ub[i][:]), reads=[f"u{i}"], writes=[f"ubf{i}"])

        def rnn_gates(t, c):
            i = c % 2

            def mm(e, i=i, c=c):
                return [e.matmul(pz[i][:, 0:T], lhsT=ga_bf[:, c * 128:(c + 1) * 128], rhs=ubf[i][:], start=True, stop=True),
                        e.matmul(pz[i][:, T:2 * T], lhsT=gx_bf[:, c * 128:(c + 1) * 128], rhs=ubf[i][:], start=True, stop=True)]
            add("pe", mm, reads=["ga_bf", "gx_bf", f"ubf{i}"], writes=[f"pz{i}"], banks=[f"PZ{i}"])
            add("act", lambda e, i=i, c=c: e.activation(out=A[c], in_=pz[i][:, 0:T], func=AF.Tanh, bias=sm("hga", c), scale=0.5),
                reads=[f"pz{i}", "sm_hga", "U_pro_done"], writes=[f"A{c}"], banks=[f"PZ{i}"])
            add("act", lambda e, i=i, c=c: e.activation(out=T1[c], in_=pz[i][:, T:2 * T], func=AF.Tanh, bias=sm("hgx", c), scale=0.5),
                reads=[f"pz{i}", "sm_hgx", "U_pro_done"], writes=[f"T1{c}"], banks=[f"PZ{i}"])
            add("act", lambda e, c=c: e.activation(out=A2[c], in_=A[c], func=AF.Exp, bias=sm("nsp8", c), scale=sm("nsp8", c)),
                reads=[f"A{c}", "sm_nsp8", "U_pro_done"], writes=[f"A2{c}"])
            add("act", lambda e, c=c: e.activation(out=A[c], in_=A[c], func=AF.Exp, bias=sm("nsp4", c), scale=sm("nsp4", c)),
                reads=[f"A{c}", "sm_nsp4"], writes=[f"A{c}"])
            add("dve", lambda e, i=i, c=c: e.scalar_tensor_tensor(out=T1[c], in0=T1[c], scalar=1.0, in1=ub[i][:], op0=ALU.add, op1=ALU.mult),
                reads=[f"T1{c}", f"u{i}"], writes=[f"T1{c}"])

        def rnn_back(t, c):
            i = c % 2
            add("act", lambda e, c=c: e.activation(out=A2[c], in_=A2[c], func=AF.Sqrt, bias=1.0, scale=-1.0), reads=[f"A2{c}"], writes=[f"A2{c}"])
            add("dve", lambda e, c=c: e.tensor_tensor(out=T1[c], in0=T1[c], in1=A2[c], op=ALU.mult), reads=[f"T1{c}", f"A2{c}"], writes=[f"T1{c}"])
            if t == 0:
                add("dve", lambda e, i=i, c=c: e.tensor_tensor_scan(out=hb[i][:], data0=A[c], data1=T1[c], initial=0.0, op0=ALU.mult, op1=ALU.add),
                    reads=[f"A{c}", f"T1{c}"], writes=[f"h{i}"])
            else:
                add("dve", lambda e, i=i, c=c: e.tensor_tensor_scan(out=hb[i][:], data0=A[c], data1=T1[c], initial=hstate[:, c:c + 1], op0=ALU.mult, op1=ALU.add),
                    reads=[f"A{c}", f"T1{c}", f"hst{c}"], writes=[f"h{i}"])
            add("pool", lambda e, i=i, c=c: e.tensor_copy(out=hstate[:, c:c + 1], in_=hb[i][:, T - 1:T]), reads=[f"h{i}"], writes=[f"hst{c}"])
            add("dve", lambda e, i=i, c=c: e.tensor_tensor(out=yT[:, c * T:(c + 1) * T], in0=hb[i][:], in1=SG[c], op=ALU.mult),
                reads=[f"h{i}", f"SG{c}"], writes=[f"yT{c}"])

        def pool_front(t, c):
            i = c % 2
            j4 = c % 4
            g = c // 2
            m = g + 1
            w_in_pair(t, i, 16 + c, 24 + c)
            pa, pb_ = pj[i][:, 0:T], pj[i][:, T:2 * T]
            E = 16 + T
            if t == 0:
                add("pool", lambda e, i=i: e.memset(xp[i][:, 0:16], 0.0), writes=[f"xp{i}"])
            else:
                add("pool", lambda e, i=i, c=c: e.tensor_copy(out=xp[i][:, 0:16], in_=hp[:, c * 16:(c + 1) * 16]), reads=[f"hp{c}"], writes=[f"xp{i}"])
            add("act", lambda e, i=i, pa=pa, c=c: e.activation(out=xp[i][:, 16:E], in_=pa, func=AF.Identity, bias=sm("pb", 16 + c), scale=1.0),
                reads=[f"pj{i}a", "sm_pb"], writes=[f"xp{i}"], banks=[f"PJ{i}"])
            add("pool", lambda e, i=i, c=c: e.tensor_copy(out=hp[:, c * 16:(c + 1) * 16], in_=xp[i][:, T:E]), reads=[f"xp{i}"], writes=[f"hp{c}"])
            add("act", lambda e, pb_=pb_, c=c, j4=j4: e.activation(out=sgp[j4][:], in_=pb_, func=AF.Tanh, bias=sm("hpb", 24 + c), scale=0.5),
                reads=[f"pj{i}b", "sm_hpb"], writes=[f"sgp{j4}"], banks=[f"PJ{i}"])
            add("pool", lambda e, j4=j4: ts1(e, sgp[j4][:], sgp[j4][:], 1.0, ALU.add), reads=[f"sgp{j4}"], writes=[f"sgp{j4}"])
            add("dve", lambda e, pb_=pb_, c=c, j4=j4: e.scalar_tensor_tensor(out=sgp[j4][:], in0=pb_, scalar=sm("pb", 24 + c), in1=sgp[j4][:], op0=ALU.add, op1=ALU.mult),
                reads=[f"pj{i}b", "sm_pb", f"sgp{j4}"], writes=[f"sgp{j4}"], banks=[f"PJ{i}"])
            bufs = [xp[i], tmpA, tmpB]
            keys = [f"xp{i}", "tmpA", "tmpB"]
            cur = 0
            lo = 0
            for k in range(1, m + 1):
                sh = 1 << (k - 1)
                nxt = 1 if cur != 1 else 2
                lo2 = lo + sh
                add("pool", lambda e, cur=cur, nxt=nxt, lo2=lo2, sh=sh: e.tensor_tensor(out=bufs[nxt][:, lo2:E], in0=bufs[cur][:, lo2:E], in1=bufs[cur][:, lo2 - sh:E - sh], op=ALU.add),
                    reads=[keys[cur]], writes=[keys[nxt]])
                cur, lo = nxt, lo2
            w = float(1 << m)
            add("dve", lambda e, cur=cur, i=i, c=c, w=w: e.scalar_tensor_tensor(out=pl[c // 2 % 2][:, (c % 2) * T:(c % 2 + 1) * T], in0=bufs[cur][:, 16:E], scalar=1.0 / w,
                                                                              in1=xp[i][:, 16:E], op0=ALU.mult, op1=ALU.subtract),
                reads=[keys[cur], f"xp{i}"], writes=[f"pl{c // 2 % 2}_{c % 2}"])
            if t == 0:
                add("dve", lambda e, cur=cur, c=c: e.tensor_tensor(out=bufs[cur][:, 16:32], in0=bufs[cur][:, 16:32], in1=invc[:, c * 16:(c + 1) * 16], op=ALU.mult),
                    reads=[keys[cur], "invc", f"pl{c // 2 % 2}_{c % 2}"], writes=[keys[cur]])
                add("dve", lambda e, cur=cur, i=i, c=c: e.tensor_tensor(out=pl[c // 2 % 2][:, (c % 2) * T:(c % 2) * T + 16], in0=bufs[cur][:, 16:32], in1=xp[i][:, 16:32], op=ALU.subtract),
                    reads=[keys[cur], f"xp{i}"], writes=[f"pl{c // 2 % 2}_{c % 2}"])

        def pool_back(t, g):
            pi = g % 2

            def mm(e, g=g, pi=pi):
                r = []
                for mm_ in range(2):
                    for kk in range(2):
                        r.append(e.matmul(pq[:, mm_ * T:(mm_ + 1) * T], lhsT=pw_bf[:, (g * 2 + kk) * 256 + mm_ * 128:(g * 2 + kk) * 256 + (mm_ + 1) * 128],
                                          rhs=pl[pi][:, kk * T:(kk + 1) * T], start=(kk == 0), stop=(kk == 1)))
                return r
            add("pe", mm, reads=["pw_bf", f"pl{pi}_0", f"pl{pi}_1"], writes=["pq"], banks=["PQ"])
            for mm_ in range(2):
                c = 2 * g + mm_
                i = c % 4
                add("dve", lambda e, c=c, i=i, mm_=mm_: e.scalar_tensor_tensor(out=yT[:, (8 + c) * T:(9 + c) * T], in0=pq[:, mm_ * T:(mm_ + 1) * T], scalar=vcol(72, c), in1=sgp[i][:],
                                                                             op0=ALU.add, op1=ALU.mult),
                    reads=["pq", "vecs", f"sgp{i}"], writes=[f"yT{8 + c}"], banks=["PQ"])

        def out_stage(src, stag, dst, dtag, t):
            for s in range(NS):
                g = t * NS + s
                sl = g % 2
                row0 = t * T + s * 128
                add("sp", lambda e, sl=sl, row0=row0: e.dma_start(out=xo[sl][:], in_=src[row0:row0 + 128, :]), reads=[f"dram_{stag}_{g}"], writes=[f"xo{sl}"], dkey=f"xo{sl}")

                def mm(e, s=s):
                    r = []
                    for hh in range(2):
                        for cc in range(16):
                            r.append(e.matmul(po[:, hh * 512:(hh + 1) * 512], lhsT=yT[:, cc * T + s * 128: cc * T + (s + 1) * 128],
                                              rhs=w_out_bf[:, cc * 1024 + hh * 512: cc * 1024 + (hh + 1) * 512], start=(cc == 0), stop=(cc == 15)))
                    return r
                add("pe", mm, reads=["w_out_bf"] + [f"yT{cc}" for cc in range(16)], writes=["po"], banks=["PO0", "PO1"])
                add("act", lambda e: e.activation(out=tmpo[:], in_=po[:], func=AF.Square, accum_out=stat[:, 8:9]), reads=["po"], writes=["tmpo", "ssO"], banks=["PO0", "PO1"])
                add("pool", lambda e: e.tensor_scalar(out=stat[:, 9:10], in0=stat[:, 8:9], scalar1=1.0 / D, scalar2=EPS, op0=ALU.mult, op1=ALU.add), reads=["ssO"], writes=["vO"])
                add("pool", lambda e: e.tensor_tensor(out=stat[:, 10:11], in0=stat[:, 9:10], in1=mhalf[:, 0:1], op=ALU.pow), reads=["vO", "mhalf"], writes=["rsO"])
                add("dve", lambda e: e.scalar_tensor_tensor(out=tmpo[:], in0=po[:], scalar=stat[:, 10:11], in1=gg[:], op0=ALU.mult, op1=ALU.mult),
                    reads=["po", "rsO", "gg"], writes=["tmpo"], banks=["PO0", "PO1"])
                add("pool", lambda e, sl=sl: e.tensor_tensor(out=xo[sl][:], in0=xo[sl][:], in1=tmpo[:], op=ALU.add), reads=[f"xo{sl}", "tmpo"], writes=[f"xo{sl}"])
                add("sp", lambda e, sl=sl, row0=row0: e.dma_start(out=dst[row0:row0 + 128, :], in_=xo[sl][:]), reads=[f"xo{sl}"], writes=[f"dram_{dtag}_{g}"], dkey=f"xo{sl}")

        for li, l in enumerate(layers):
            src = x_in if li == 0 else mids[li - 1]
            dst = out_d if li == NL - 1 else mids[li]
            stag = "x" if li == 0 else f"mid{li - 1}"
            dtag = "out" if li == NL - 1 else f"mid{li}"
            prologue(l)
            stage_a(src, stag, 0)
            for t in range(nt):
                for step in range(8 + 1):
                    if step < 8:
                        rnn_front(t, step)
                    if step >= 1:
                        rnn_gates(t, step - 1)
                if t > 0:
                    out_stage(src, stag, dst, dtag, t - 1)
                for c in range(8):
                    pool_front(t, c)
                    if c % 2 == 1 and c >= 3:
                        pool_back(t, c // 2 - 1)
                pool_back(t, 3)
                if t + 1 < nt:
                    stage_a(src, stag, t + 1)
                for c in range(8):
                    rnn_back(t, c)
            out_stage(src, stag, dst, dtag, nt - 1)
            barrier(U_MAIN_KEYS, "U_main_done")

        S.emit(nc, block, sems, dsems, final_wait_keys=["xo0", "xo1"])
    return nc


def _host_layout(inp):
    f = np.float32
    L = DEPTH

    def pk(v):
        return np.ascontiguousarray(np.asarray(v, f).reshape(L, 8, 128).transpose(0, 2, 1))

    ada_w = np.asarray(inp["ada_w"], f).reshape(L, 2, 4, 128, 6, 512)
    ada_w = np.ascontiguousarray(ada_w.transpose(0, 4, 1, 3, 2, 5)).reshape(L * 6 * 2, 128, 2048)
    w_in = np.asarray(inp["w_in"], f).reshape(L, 2, 4, 128, 8, 512)
    w_in = np.ascontiguousarray(w_in.transpose(0, 4, 1, 3, 2, 5)).reshape(L * 8 * 2, 128, 2048)
    w_out = np.asarray(inp["w_out"], f).reshape(L, 4, 2, 2, 128, 1024)
    w_out = np.ascontiguousarray(w_out.transpose(0, 1, 2, 4, 3, 5)).reshape(L * 4 * 2, 128, 2048)
    ga_w = np.ascontiguousarray(np.asarray(inp["gate_a_w"], f).transpose(0, 2, 1, 3)).reshape(L, 128, 1024)
    gx_w = np.ascontiguousarray(np.asarray(inp["gate_x_w"], f).transpose(0, 2, 1, 3)).reshape(L, 128, 1024)
    pw = np.asarray(inp["pool_w"], f).reshape(L, 4, 2, 128, 256)
    pw = np.ascontiguousarray(pw.transpose(0, 3, 1, 2, 4)).reshape(L, 128, 2048)
    conv_w = np.asarray(inp["conv_w"], f).reshape(L, 4, 8, 128)
    conv_w = np.ascontiguousarray(conv_w.transpose(0, 3, 2, 1)).reshape(L, 128, 32)
    vecs = np.concatenate([pk(inp["pre_norm_g"]), conv_w, pk(inp["conv_b"]),
                           pk(np.asarray(inp["gate_a_b"], f).reshape(L, 1024)), pk(np.asarray(inp["gate_x_b"], f).reshape(L, 1024)),
                           pk(inp["lru_lambda"]), pk(np.asarray(inp["pool_b"], f).reshape(L, 1024)), pk(inp["pool_scale"])], axis=2)
    assert vecs.shape == (L, 128, NV)
    ada_b_bc = np.ascontiguousarray(np.broadcast_to(np.asarray(inp["ada_b"], f)[:, None, :], (L, 128, 3 * D)))
    post_g_bc = np.ascontiguousarray(np.broadcast_to(np.asarray(inp["post_norm_g"], f)[:, None, :], (L, 128, D)))
    invcnt = np.zeros((128, 8, 16), f)
    for c in range(8):
        w = 2 << (c // 2)
        invcnt[:, c, :] = 1.0 / np.minimum(np.arange(16) + 1, w)
    shared = {"ada_w": ada_w, "ada_b_bc": ada_b_bc, "post_g_bc": post_g_bc, "w_in": w_in, "w_out": w_out, "ga_w": ga_w, "gx_w": gx_w,
              "pool_w": pw, "vecs": np.ascontiguousarray(vecs), "ident": np.eye(128, dtype=f), "invcnt": invcnt.reshape(128, 128)}
    c = np.asarray(inp["c"], f)
    cbcs = []
    for b in range(c.shape[0]):
        cb = c[b].reshape(8, 128).T
        cbcs.append(np.ascontiguousarray(np.broadcast_to(cb[:, :, None], (128, 8, 128))).reshape(128, 1024))
    return shared, cbcs


LAUNCH_GROUPS = [[0], [1]]


def kernel(**inputs):
    x = np.asarray(inputs["x"], np.float32)
    nb = x.shape[0]
    shared, cbcs = _host_layout(inputs)
    cur = [np.ascontiguousarray(x[b]) for b in range(nb)]
    for grp in LAUNCH_GROUPS:
        nc = build_nc(list(grp))
        in_maps = [dict(shared, x=cur[b], cbc=cbcs[b]) for b in range(nb)]
        res = run_bass_kernel_spmd(nc, in_maps, core_ids=list(range(nb)))
        cur = [np.asarray(res.results[b]["out"], np.float32) for b in range(nb)]
    return np.stack(cur, axis=0)
```

```python
import numpy as np
import concourse.bass as bass
import concourse.mybir as mybir
from concourse.alu_op_type import AluOpType as ALU
from concourse.bass_utils import run_bass_kernel_spmd

F32 = mybir.dt.float32
BF16 = mybir.dt.bfloat16
AF = mybir.ActivationFunctionType

ENGS = ("pe", "act", "dve", "pool", "sp")


class _Op:
    __slots__ = ("eng", "fn", "reads", "writes", "deps", "signal", "sem", "val", "dkey", "ndma", "banks")

    def __init__(self, eng, fn, reads, writes, dkey, ndma, banks=()):
        self.eng, self.fn, self.reads, self.writes = eng, fn, tuple(reads), tuple(writes)
        self.banks = tuple(banks)
        self.deps = []
        self.signal = False
        self.sem = None
        self.val = 0
        self.dkey = dkey
        self.ndma = ndma


class Sched:
    def __init__(self, same_eng_wait=True):
        self.ops = []
        self.same_eng_wait = same_eng_wait

    def add(self, eng, fn, reads=(), writes=(), dkey=None, ndma=1, banks=()):
        assert eng in ENGS
        if eng == "sp" and dkey is None:
            dkey = writes[0] if writes else reads[0]
        self.ops.append(_Op(eng, fn, reads, writes, dkey, ndma, banks))

    def analyze(self):
        last_w = {}
        readers = {}
        bank_last = {}
        pos_in_eng = {}
        cnt = {e: 0 for e in ENGS}
        for i, op in enumerate(self.ops):
            pos_in_eng[i] = cnt[op.eng]
            cnt[op.eng] += 1
            deps = set()
            for r in op.reads:
                if r in last_w:
                    deps.add(last_w[r])
            for w in op.writes:
                if w in last_w:
                    deps.add(last_w[w])
                for j in readers.get(w, ()):
                    deps.add(j)
            for b in op.banks:
                bl = bank_last.setdefault(b, {})
                for e2, j in bl.items():
                    if e2 != op.eng:
                        deps.add(j)
                bl[op.eng] = i
            deps.discard(i)
            best = {}
            keep = []
            for j in deps:
                e = self.ops[j].eng
                if e == "sp":
                    keep.append(j)
                    continue
                if e == op.eng:
                    if e == "pe":
                        continue
                    if not self.same_eng_wait:
                        continue
                if e not in best or j > best[e]:
                    best[e] = j
            keep.extend(best.values())
            op.deps = sorted(keep)
            for j in op.deps:
                self.ops[j].signal = True
            for w in op.writes:
                last_w[w] = i
                readers[w] = []
            for r in op.reads:
                readers.setdefault(r, []).append(i)

    def emit(self, nc, block, sems, dma_sems, final_wait_keys=()):
        self.analyze()
        cnt = {e: 0 for e in ENGS}
        dcnt = {}
        for op in self.ops:
            if op.eng == "sp":
                dcnt[op.dkey] = dcnt.get(op.dkey, 0) + 16 * op.ndma
                op.sem, op.val = dma_sems[op.dkey], dcnt[op.dkey]
            elif op.signal:
                cnt[op.eng] += 1
                op.sem, op.val = sems[op.eng], cnt[op.eng]
        ops = self.ops

        def run(eng_name, e):
            waited = {}
            for op in ops:
                if op.eng != eng_name:
                    continue
                for j in op.deps:
                    d = ops[j]
                    key = id(d.sem)
                    if waited.get(key, 0) >= d.val:
                        continue
                    e.wait_ge(d.sem, d.val)
                    waited[key] = d.val
                res = op.fn(e)
                if op.eng == "sp":
                    lst = res if isinstance(res, (list, tuple)) else [res]
                    assert len(lst) == op.ndma, (len(lst), op.ndma)
                    for ins in lst:
                        ins.then_inc(op.sem, 16)
                elif op.signal:
                    ins = res[-1] if isinstance(res, (list, tuple)) else res
                    ins.then_inc(op.sem, 1)
            if eng_name == "sp":
                for k in final_wait_keys:
                    if k in dcnt:
                        e.wait_ge(dma_sems[k], dcnt[k])

        @block.tensor
        def _(e):
            run("pe", e)

        @block.scalar
        def _(e):
            run("act", e)

        @block.vector
        def _(e):
            run("dve", e)

        @block.gpsimd
        def _(e):
            run("pool", e)

        @block.sync
        def _(e):
            run("sp", e)


D = 1024
SEQ = 4096
NB = 8
DEPTH = 2
T = 256
NS = T // 128
NT = SEQ // T
EPS = 1e-6
KC = 8
SAME_ENG_WAIT = True
NV = 88
LOG1P_COEF = [1.0 / (2 * k + 1) for k in range(8)]


def build_nc(layers, seq=SEQ):
    from contextlib import ExitStack
    nt = seq // T
    nc = bass.Bass("TRN2", target_bir_lowering=False)
    NL = len(layers)

    def din(name, shape, dt=F32):
        return nc.dram_tensor(name, shape, dt, kind="ExternalInput").ap()

    x_in = din("x", [seq, D])
    cbc_d = din("cbc", [128, KC * 128])
    ada_w_d = din("ada_w", [DEPTH * 6 * 2, 128, 2048])
    ada_b_d = din("ada_b_bc", [DEPTH, 128, 3 * D])
    postg_d = din("post_g_bc", [DEPTH, 128, D])
    w_in_d = din("w_in", [DEPTH * 8 * 2, 128, 2048])
    w_out_d = din("w_out", [DEPTH * 4 * 2, 128, 2048])
    ga_w_d = din("ga_w", [DEPTH, 128, 1024])
    gx_w_d = din("gx_w", [DEPTH, 128, 1024])
    pw_d = din("pool_w", [DEPTH, 128, 2048])
    vecs_d = din("vecs", [DEPTH, 128, NV])
    ident_d = din("ident", [128, 128])
    invc_d = din("invcnt", [128, 8 * 16])
    out_d = nc.dram_tensor("out", [seq, D], F32, kind="ExternalOutput").ap()
    mids = [nc.dram_tensor(f"xmid{i}", [seq, D], F32, kind="Internal").ap() for i in range(NL - 1)]

    with ExitStack() as es:
        def sb(name, shape, dt=F32):
            return es.enter_context(nc.sbuf_tensor(name, shape, dt))

        def ps(name, shape, dt=F32):
            return es.enter_context(nc.psum_tensor(name, shape, dt))

        w_in_bf = sb("w_in_bf", [128, KC * 4096], BF16)
        w_out_bf = sb("w_out_bf", [128, 16 * 1024], BF16)
        ga_bf = sb("ga_bf", [128, 1024], BF16)
        gx_bf = sb("gx_bf", [128, 1024], BF16)
        pw_bf = sb("pw_bf", [128, 2048], BF16)
        U = sb("U", [128, 8192], F32)
        xs = [sb(f"xs{i}", [128, D]) for i in range(2)]
        xn = [sb(f"xn{i}", [128, D], BF16) for i in range(2)]
        hT = [sb(f"hT{i}", [128, KC * T], BF16) for i in range(2)]
        xr = [sb(f"xr{i}", [128, 4 + T]) for i in range(2)]
        ub = [sb(f"u{i}", [128, T]) for i in range(2)]
        ubf = [sb(f"ubf{i}", [128, T], BF16) for i in range(2)]
        hb = [sb(f"h{i}", [128, T]) for i in range(2)]
        xp = [sb(f"xp{i}", [128, 16 + T]) for i in range(2)]
        tmpA = sb("tmpA", [128, 16 + T])
        tmpB = sb("tmpB", [128, 16 + T])
        pl = [sb(f"pl{i}", [128, 2 * T], BF16) for i in range(2)]
        sgp = [sb(f"sgp{i}", [128, T]) for i in range(4)]
        yT = sb("yT", [128, 16 * T], BF16)
        xo = [sb(f"xo{i}", [128, D]) for i in range(2)]
        tmpo = sb("tmpo", [128, D])
        gg = sb("gg", [128, D])
        vecs = sb("vecs_sb", [128, NV])
        identf = sb("identf", [128, 128])
        identb = sb("identb", [128, 128], BF16)
        invc = sb("invc", [128, 128])
        mhalf = sb("mhalf", [128, 8])
        small = sb("small", [128, 256])
        stat = sb("stat", [128, 64])
        hx = sb("hx", [128, 8 * 4])
        hp = sb("hp", [128, 8 * 16])
        hstate = sb("hstate", [128, 8])

        def Uv(off, n):
            return U[:, off:off + n]
        A = [Uv(c * T, T) for c in range(8)]
        A2 = [Uv(2048 + c * T, T) for c in range(8)]
        T1 = [Uv(4096 + c * T, T) for c in range(8)]
        SG = [Uv(6144 + c * T, T) for c in range(8)]
        stage = [Uv(0, 2048), Uv(2048, 2048)]
        mod_bc = Uv(4096, 3072)
        cact = Uv(7168, 1024)

        SM = {}
        _o = [0]

        def smalloc(name, n):
            SM[name] = (_o[0], n)
            _o[0] += n
        for nm in ("shift", "scale", "gs", "nsp4", "nsp8", "hga", "hgx"):
            smalloc(nm, 8)
        smalloc("ws", 16)
        smalloc("pb", 32)
        smalloc("hpb", 32)
        for nm in ("sp_e", "sp_z", "sp_z2", "sp_p", "sp_t", "sp_ax", "sp_mx"):
            smalloc(nm, 8)
        assert _o[0] <= 256

        def sm(name, c=None, n=1):
            o, w = SM[name]
            if c is None:
                return small[:, o:o + w]
            return small[:, o + c:o + c + n]

        def vcol(o, c=None, n=8):
            if c is None:
                return vecs[:, o:o + n]
            return vecs[:, o + c:o + c + 1]

        pj = [ps(f"pj{i}", [128, 2 * T]) for i in range(2)]
        pz = [ps(f"pz{i}", [128, 2 * T]) for i in range(2)]
        pq = ps("pq", [128, 2 * T])
        ptr = ps("ptr", [128, KC * 128], BF16)
        po = ps("po", [128, D])

        sems = {e: es.enter_context(nc.semaphore("s_" + e)) for e in ENGS if e != "sp"}
        dkeys = ["xs0", "xs1", "xo0", "xo1", "stage0", "stage1", "identf", "invc", "cact", "vecs", "gg"] + [f"mod_bc{i}" for i in range(6)]
        dsems = {k: es.enter_context(nc.semaphore("d_" + k)) for k in dkeys}
        block = es.enter_context(nc.Block())
        S = Sched(same_eng_wait=SAME_ENG_WAIT)
        add = S.add

        def ts1(e, out, in0, s1, op0=ALU.mult):
            if op0 == ALU.mult:
                return e.tensor_scalar(out=out, in0=in0, scalar1=s1, scalar2=0.0, op0=ALU.mult, op1=ALU.add)
            assert op0 == ALU.add
            return e.tensor_scalar(out=out, in0=in0, scalar1=s1, scalar2=1.0, op0=ALU.add, op1=ALU.mult)

        add("sp", lambda e: e.dma_start(out=identf[:], in_=ident_d), writes=["identf"])
        add("sp", lambda e: e.dma_start(out=invc[:], in_=invc_d), writes=["invc"])
        add("dve", lambda e: e.tensor_copy(out=identb[:], in_=identf[:]), reads=["identf"], writes=["identb"])
        add("pool", lambda e: e.memset(mhalf[:], -0.5), writes=["mhalf"])
        cnt = {"stage": 0, "cast": 0}

        def barrier(keys, token):
            add("pool", lambda e: e.memset(stat[:, 63:64], 0.0), writes=list(keys) + [token, "stat63"])

        U_MAIN_KEYS = [f"A{c}" for c in range(8)] + [f"A2{c}" for c in range(8)] + \
                      [f"T1{c}" for c in range(8)] + [f"SG{c}" for c in range(8)]
        MODK = [f"mod_bc{i}" for i in range(6)]
        U_PRO_KEYS = ["stage0", "stage1", "cact"] + MODK

        def stage_load(src_ap, width=2048):
            i = cnt["stage"] % 2
            cnt["stage"] += 1
            add("sp", lambda e: e.dma_start(out=stage[i][:, 0:width], in_=src_ap), reads=["U_main_done"], writes=[f"stage{i}"], dkey=f"stage{i}")
            return i

        def scaled_cast(dst, src, scal, skey, reads, writes):
            cnt["cast"] += 1
            eng = ("act", "dve", "pool")[cnt["cast"] % 3]
            if eng == "act":
                add("act", lambda e: e.activation(out=dst, in_=src, func=AF.Copy, scale=scal), reads=reads + [skey], writes=writes)
            else:
                add(eng, lambda e: ts1(e, dst, src, scal), reads=reads + [skey], writes=writes)

        def prologue(l):
            add("sp", lambda e: e.dma_start(out=cact, in_=cbc_d), reads=["U_main_done"], writes=["cact"])
            add("act", lambda e: e.activation(out=mod_bc[:, 0:1024], in_=cact, func=AF.Tanh, scale=0.5), reads=["cact", "U_main_done"], writes=["mod_bc0", "mod_bc1"])
            add("dve", lambda e: e.scalar_tensor_tensor(out=cact, in0=mod_bc[:, 0:1024], scalar=1.0, in1=cact, op0=ALU.add, op1=ALU.mult),
                reads=["mod_bc0", "mod_bc1", "cact"], writes=["cact"])
            add("dve", lambda e: ts1(e, cact, cact, 0.5), reads=["cact"], writes=["cact"])
            add("sp", lambda e: e.dma_start(out=vecs[:], in_=vecs_d[l]), writes=["vecs"])
            for nb in range(6):
                for kh in range(2):
                    si = stage_load(ada_w_d[(l * 6 + nb) * 2 + kh])

                    def mm(e, si=si, kh=kh):
                        r = []
                        for k4 in range(4):
                            kc = kh * 4 + k4
                            r.append(e.matmul(po[:, 0:512], lhsT=cact[:, kc * 128:(kc + 1) * 128], rhs=stage[si][:, k4 * 512:(k4 + 1) * 512],
                                              start=(kc == 0), stop=(kc == 7)))
                        return r
                    add("pe", mm, reads=[f"stage{si}", "cact"], writes=["po"], banks=["PO0"])
                add("sp", lambda e, nb=nb: e.dma_start(out=mod_bc[:, nb * 512:(nb + 1) * 512], in_=ada_b_d[l][:, nb * 512:(nb + 1) * 512]),
                    reads=["U_main_done"], writes=[f"mod_bc{nb}"])
                add("dve", lambda e, nb=nb: e.tensor_tensor(out=mod_bc[:, nb * 512:(nb + 1) * 512], in0=po[:, 0:512], in1=mod_bc[:, nb * 512:(nb + 1) * 512], op=ALU.add),
                    reads=["po", f"mod_bc{nb}"], writes=[f"mod_bc{nb}"], banks=["PO0"])
            for which, base in (("shift", 0), ("scale", 1024)):
                for kc in range(8):
                    nbk = (base + kc * 128) // 512
                    add("dve", lambda e, which=which, base=base, kc=kc: e.scalar_tensor_tensor(
                        out=tmpA[:, 0:128], in0=mod_bc[:, base + kc * 128:base + (kc + 1) * 128], scalar=1.0, in1=identf[:],
                        op0=ALU.mult, op1=ALU.mult, accum_out=sm(which, kc)),
                        reads=[f"mod_bc{nbk}", "identf"], writes=["tmpA", f"sm_{which}"])
            add("dve", lambda e: e.scalar_tensor_tensor(out=sm("gs"), in0=sm("scale"), scalar=1.0, in1=vcol(0), op0=ALU.add, op1=ALU.mult),
                reads=["sm_scale", "vecs"], writes=["sm_gs"])
            add("sp", lambda e: e.dma_start(out=gg[:], in_=postg_d[l]), writes=["gg"])
            for hh in range(2):
                add("pool", lambda e, hh=hh: e.tensor_tensor(out=gg[:, hh * 512:(hh + 1) * 512], in0=gg[:, hh * 512:(hh + 1) * 512],
                                                            in1=mod_bc[:, 2048 + hh * 512:2048 + (hh + 1) * 512], op=ALU.mult),
                    reads=["gg", f"mod_bc{4 + hh}"], writes=["gg"])
            lam = vcol(64)
            add("dve", lambda e: e.scalar_tensor_tensor(out=sm("sp_ax"), in0=lam, scalar=-1.0, in1=lam, op0=ALU.mult, op1=ALU.max), reads=["vecs"], writes=["sm_sp_ax"])
            add("act", lambda e: e.activation(out=sm("sp_e"), in_=sm("sp_ax"), func=AF.Exp, scale=-1.0), reads=["sm_sp_ax"], writes=["sm_sp_e"])
            add("dve", lambda e: ts1(e, sm("sp_t"), sm("sp_e"), 2.0, ALU.add), reads=["sm_sp_e"], writes=["sm_sp_t"])
            add("dve", lambda e: e.reciprocal(out=sm("sp_t"), in_=sm("sp_t")), reads=["sm_sp_t"], writes=["sm_sp_t"])
            add("dve", lambda e: e.tensor_tensor(out=sm("sp_z"), in0=sm("sp_e"), in1=sm("sp_t"), op=ALU.mult), reads=["sm_sp_e", "sm_sp_t"], writes=["sm_sp_z"])
            add("dve", lambda e: e.tensor_tensor(out=sm("sp_z2"), in0=sm("sp_z"), in1=sm("sp_z"), op=ALU.mult), reads=["sm_sp_z"], writes=["sm_sp_z2"])
            add("dve", lambda e: e.tensor_scalar(out=sm("sp_p"), in0=sm("sp_z2"), scalar1=LOG1P_COEF[7], scalar2=LOG1P_COEF[6], op0=ALU.mult, op1=ALU.add),
                reads=["sm_sp_z2"], writes=["sm_sp_p"])
            for k in (5, 4, 3, 2, 1, 0):
                add("dve", lambda e: e.tensor_tensor(out=sm("sp_p"), in0=sm("sp_p"), in1=sm("sp_z2"), op=ALU.mult), reads=["sm_sp_p", "sm_sp_z2"], writes=["sm_sp_p"])
                add("dve", lambda e, k=k: ts1(e, sm("sp_p"), sm("sp_p"), LOG1P_COEF[k], ALU.add), reads=["sm_sp_p"], writes=["sm_sp_p"])
            add("dve", lambda e: e.tensor_tensor(out=sm("sp_p"), in0=sm("sp_p"), in1=sm("sp_z"), op=ALU.mult), reads=["sm_sp_p", "sm_sp_z"], writes=["sm_sp_p"])
            add("dve", lambda e: e.tensor_scalar(out=sm("sp_mx"), in0=lam, scalar1=-1.0, scalar2=0.0, op0=ALU.mult, op1=ALU.max), reads=["vecs"], writes=["sm_sp_mx"])
            add("dve", lambda e: e.scalar_tensor_tensor(out=sm("sp_t"), in0=sm("sp_p"), scalar=2.0, in1=sm("sp_mx"), op0=ALU.mult, op1=ALU.add),
                reads=["sm_sp_p", "sm_sp_mx", "sm_sp_t"], writes=["sm_sp_t"])
            add("dve", lambda e: ts1(e, sm("nsp4"), sm("sp_t"), -4.0), reads=["sm_sp_t"], writes=["sm_nsp4"])
            add("dve", lambda e: ts1(e, sm("nsp8"), sm("sp_t"), -8.0), reads=["sm_sp_t"], writes=["sm_nsp8"])
            add("dve", lambda e: ts1(e, sm("hga"), vcol(48), 0.5), reads=["vecs"], writes=["sm_hga"])
            add("dve", lambda e: ts1(e, sm("hgx"), vcol(56), 0.5), reads=["vecs"], writes=["sm_hgx"])
            add("pool", lambda e: e.memset(sm("ws", 0, 8), 0.25), writes=["sm_ws"])
            add("dve", lambda e: ts1(e, sm("ws", 8, 8), vcol(80), 0.5), reads=["vecs", "sm_ws"], writes=["sm_ws"])
            for cb in range(8):
                for kh in range(2):
                    si = stage_load(w_in_d[(l * 8 + cb) * 2 + kh])
                    for k4 in range(4):
                        kc = kh * 4 + k4
                        dst = w_in_bf[:, kc * 4096 + cb * 512: kc * 4096 + (cb + 1) * 512]
                        src = stage[si][:, k4 * 512:(k4 + 1) * 512]
                        scaled_cast(dst, src, sm("gs", kc), "sm_gs", [f"stage{si}"], [f"w_in_bf{cb}"])

                    def mmb(e, si=si, kh=kh, cb=cb):
                        r = []
                        for jb in range(4):
                            col = 512 + kh * 32 + cb * 4 + jb
                            for k4 in range(4):
                                kc = kh * 4 + k4
                                r.append(e.matmul(po[:, col:col + 1],
                                                  lhsT=stage[si][:, k4 * 512 + jb * 128: k4 * 512 + (jb + 1) * 128],
                                                  rhs=sm("shift", kc), start=(k4 == 0), stop=(k4 == 3)))
                        return r
                    add("pe", mmb, reads=[f"stage{si}", "sm_shift"], writes=["po_b"], banks=["PO1"])
            add("dve", lambda e: e.tensor_copy(out=sm("hpb"), in_=po[:, 512:544]), reads=["po_b", "po"], writes=["sm_hpb"], banks=["PO1"])
            add("dve", lambda e: e.tensor_tensor(out=sm("pb"), in0=po[:, 544:576], in1=sm("hpb"), op=ALU.add), reads=["po_b", "po", "sm_hpb"], writes=["sm_pb"], banks=["PO1"])
            add("dve", lambda e: ts1(e, sm("hpb"), sm("pb"), 0.5), reads=["sm_pb", "sm_hpb"], writes=["sm_hpb"])
            for rb in range(4):
                for hh in range(2):
                    si = stage_load(w_out_d[(l * 4 + rb) * 2 + hh])
                    for c2 in range(2):
                        cc = rb * 4 + hh * 2 + c2
                        scaled_cast(w_out_bf[:, cc * 1024:(cc + 1) * 1024], stage[si][:, c2 * 1024:(c2 + 1) * 1024], sm("ws", cc), "sm_ws",
                                    [f"stage{si}"], ["w_out_bf"])
            si = stage_load(ga_w_d[l], 1024)
            add("pool", lambda e, si=si: e.tensor_copy(out=ga_bf[:], in_=stage[si][:, 0:1024]), reads=[f"stage{si}"], writes=["ga_bf"])
            si = stage_load(gx_w_d[l], 1024)
            add("pool", lambda e, si=si: e.tensor_copy(out=gx_bf[:], in_=stage[si][:, 0:1024]), reads=[f"stage{si}"], writes=["gx_bf"])
            si = stage_load(pw_d[l], 2048)
            add("dve", lambda e, si=si: e.tensor_copy(out=pw_bf[:], in_=stage[si][:, 0:2048]), reads=[f"stage{si}"], writes=["pw_bf"])
            barrier(U_PRO_KEYS, "U_pro_done")

        def stage_a(src, stag, t):
            hs = t % 2
            for s in range(NS):
                g = t * NS + s
                sl = g % 2
                row0 = t * T + s * 128
                add("sp", lambda e, sl=sl, row0=row0: e.dma_start(out=xs[sl][:], in_=src[row0:row0 + 128, :]), reads=[f"dram_{stag}_{g}"], writes=[f"xs{sl}"], dkey=f"xs{sl}")
                add("act", lambda e, sl=sl: e.activation(out=xn[sl][:], in_=xs[sl][:], func=AF.Square, accum_out=stat[:, sl:sl + 1]),
                    reads=[f"xs{sl}"], writes=[f"xn{sl}", f"ssA{sl}"])
                add("pool", lambda e, sl=sl: e.tensor_scalar(out=stat[:, 2 + sl:3 + sl], in0=stat[:, sl:sl + 1], scalar1=1.0 / D, scalar2=EPS, op0=ALU.mult, op1=ALU.add),
                    reads=[f"ssA{sl}"], writes=[f"vA{sl}"])
                add("pool", lambda e, sl=sl: e.tensor_tensor(out=stat[:, 4 + sl:5 + sl], in0=stat[:, 2 + sl:3 + sl], in1=mhalf[:, 0:1], op=ALU.pow),
                    reads=[f"vA{sl}", "mhalf"], writes=[f"rsA{sl}"])
                add("act", lambda e, sl=sl: e.activation(out=xn[sl][:], in_=xs[sl][:], func=AF.Copy, scale=stat[:, 4 + sl:5 + sl]),
                    reads=[f"xs{sl}", f"rsA{sl}"], writes=[f"xn{sl}"])

                def tr(e, sl=sl):
                    return [e.transpose(out=ptr[:, kc * 128:(kc + 1) * 128], in_=xn[sl][:, kc * 128:(kc + 1) * 128], identity=identb[:]) for kc in range(KC)]
                add("pe", tr, reads=[f"xn{sl}", "identb"], writes=["ptr"], banks=["PTR"])
                add("dve", lambda e, hs=hs, s=s: e.tensor_copy(
                    out=hT[hs][:, :].rearrange("p (k t) -> p k t", k=KC)[:, :, s * 128:(s + 1) * 128],
                    in_=ptr[:, :].rearrange("p (k t) -> p k t", k=KC)),
                    reads=["ptr"], writes=[f"hT{hs}_{s}"], banks=["PTR"])

        def w_in_pair(t, i, q1, q2):
            hs = t % 2

            def mm(e, i=i, q1=q1, q2=q2, hs=hs):
                r = []
                for half, q in ((0, q1), (1, q2)):
                    for kc in range(KC):
                        r.append(e.matmul(pj[i][:, half * T:(half + 1) * T], lhsT=w_in_bf[:, kc * 4096 + q * 128: kc * 4096 + (q + 1) * 128],
                                          rhs=hT[hs][:, kc * T:(kc + 1) * T], start=(kc == 0), stop=(kc == KC - 1)))
                return r
            add("pe", mm, reads=[f"w_in_bf{q1 // 4}", f"w_in_bf{q2 // 4}"] + [f"hT{hs}_{s}" for s in range(NS)], writes=[f"pj{i}a", f"pj{i}b"], banks=[f"PJ{i}"])

        def rnn_front(t, c):
            i = c % 2
            w_in_pair(t, i, c, 8 + c)
            pa, pb_ = pj[i][:, 0:T], pj[i][:, T:2 * T]
            if t == 0:
                add("pool", lambda e, i=i: e.memset(xr[i][:, 0:3], 0.0), writes=[f"xr{i}"])
            else:
                add("pool", lambda e, i=i, c=c: e.tensor_copy(out=xr[i][:, 0:3], in_=hx[:, c * 4:c * 4 + 3]), reads=[f"hx{c}"], writes=[f"xr{i}"])
            add("act", lambda e, i=i, pa=pa, c=c: e.activation(out=xr[i][:, 3:3 + T], in_=pa, func=AF.Identity, bias=sm("pb", c), scale=1.0),
                reads=[f"pj{i}a", "sm_pb"], writes=[f"xr{i}"], banks=[f"PJ{i}"])
            add("pool", lambda e, i=i, c=c: e.tensor_copy(out=hx[:, c * 4:c * 4 + 3], in_=xr[i][:, T:T + 3]), reads=[f"xr{i}"], writes=[f"hx{c}"])
            add("act", lambda e, pb_=pb_, c=c: e.activation(out=SG[c], in_=pb_, func=AF.Tanh, bias=sm("hpb", 8 + c), scale=0.5),
                reads=[f"pj{i}b", "sm_hpb", "U_pro_done"], writes=[f"SG{c}"], banks=[f"PJ{i}"])
            add("pool", lambda e, c=c: ts1(e, SG[c], SG[c], 1.0, ALU.add), reads=[f"SG{c}"], writes=[f"SG{c}"])
            add("dve", lambda e, i=i, c=c: e.tensor_scalar(out=ub[i][:], in0=xr[i][:, 0:T], scalar1=vcol(8, c * 4 + 0), scalar2=vcol(40, c), op0=ALU.mult, op1=ALU.add),
                reads=[f"xr{i}", "vecs"], writes=[f"u{i}"])
            for k in (1, 2, 3):
                add("dve", lambda e, i=i, c=c, k=k: e.scalar_tensor_tensor(out=ub[i][:], in0=xr[i][:, k:k + T], scalar=vcol(8, c * 4 + k), in1=ub[i][:],
                                                                           op0=ALU.mult, op1=ALU.add),
                    reads=[f"xr{i}", "vecs", f"u{i}"], writes=[f"u{i}"])
            add("dve", lambda e, pb_=pb_, c=c: e.scalar_tensor_tensor(out=SG[c], in0=pb_, scalar=sm("pb", 8 + c), in1=SG[c], op0=ALU.add, op1=ALU.mult),
                reads=[f"pj{i}b", "sm_pb", f"SG{c}"], writes=[f"SG{c}"], banks=[f"PJ{i}"])
            add("pool", lambda e, i=i: e.tensor_copy(out=ubf[i][:], in_=ub[i][:]), reads=[f"u{i}"], writes=[f"ubf{i}"])

        def rnn_gates(t, c):
            i = c % 2

            def mm(e, i=i, c=c):
                return [e.matmul(pz[i][:, 0:T], lhsT=ga_bf[:, c * 128:(c + 1) * 128], rhs=ubf[i][:], start=True, stop=True),
                        e.matmul(pz[i][:, T:2 * T], lhsT=gx_bf[:, c * 128:(c + 1) * 128], rhs=ubf[i][:], start=True, stop=True)]
            add("pe", mm, reads=["ga_bf", "gx_bf", f"ubf{i}"], writes=[f"pz{i}"], banks=[f"PZ{i}"])
            add("act", lambda e, i=i, c=c: e.activation(out=A[c], in_=pz[i][:, 0:T], func=AF.Tanh, bias=sm("hga", c), scale=0.5),
                reads=[f"pz{i}", "sm_hga", "U_pro_done"], writes=[f"A{c}"], banks=[f"PZ{i}"])
            add("act", lambda e, i=i, c=c: e.activation(out=T1[c], in_=pz[i][:, T:2 * T], func=AF.Tanh, bias=sm("hgx", c), scale=0.5),
                reads=[f"pz{i}", "sm_hgx", "U_pro_done"], writes=[f"T1{c}"], banks=[f"PZ{i}"])
            add("act", lambda e, c=c: e.activation(out=A2[c], in_=A[c], func=AF.Exp, bias=sm("nsp8", c), scale=sm("nsp8", c)),
                reads=[f"A{c}", "sm_nsp8", "U_pro_done"], writes=[f"A2{c}"])
            add("act", lambda e, c=c: e.activation(out=A[c], in_=A[c], func=AF.Exp, bias=sm("nsp4", c), scale=sm("nsp4", c)),
                reads=[f"A{c}", "sm_nsp4"], writes=[f"A{c}"])
            add("dve", lambda e, i=i, c=c: e.scalar_tensor_tensor(out=T1[c], in0=T1[c], scalar=1.0, in1=ub[i][:], op0=ALU.add, op1=ALU.mult),
                reads=[f"T1{c}", f"u{i}"], writes=[f"T1{c}"])

        def rnn_back(t, c):
            i = c % 2
            add("act", lambda e, c=c: e.activation(out=A2[c], in_=A2[c], func=AF.Sqrt, bias=1.0, scale=-1.0), reads=[f"A2{c}"], writes=[f"A2{c}"])
            add("dve", lambda e, c=c: e.tensor_tensor(out=T1[c], in0=T1[c], in1=A2[c], op=ALU.mult), reads=[f"T1{c}", f"A2{c}"], writes=[f"T1{c}"])
            if t == 0:
                add("dve", lambda e, i=i, c=c: e.tensor_tensor_scan(out=hb[i][:], data0=A[c], data1=T1[c], initial=0.0, op0=ALU.mult, op1=ALU.add),
                    reads=[f"A{c}", f"T1{c}"], writes=[f"h{i}"])
            else:
                add("dve", lambda e, i=i, c=c: e.tensor_tensor_scan(out=hb[i][:], data0=A[c], data1=T1[c], initial=hstate[:, c:c + 1], op0=ALU.mult, op1=ALU.add),
                    reads=[f"A{c}", f"T1{c}", f"hst{c}"], writes=[f"h{i}"])
            add("pool", lambda e, i=i, c=c: e.tensor_copy(out=hstate[:, c:c + 1], in_=hb[i][:, T - 1:T]), reads=[f"h{i}"], writes=[f"hst{c}"])
            add("dve", lambda e, i=i, c=c: e.tensor_tensor(out=yT[:, c * T:(c + 1) * T], in0=hb[i][:], in1=SG[c], op=ALU.mult),
                reads=[f"h{i}", f"SG{c}"], writes=[f"yT{c}"])

        def pool_front(t, c):
            i = c % 2
            j4 = c % 4
            g = c // 2
            m = g + 1
            w_in_pair(t, i, 16 + c, 24 + c)
            pa, pb_ = pj[i][:, 0:T], pj[i][:, T:2 * T]
            E = 16 + T
            if t == 0:
                add("pool", lambda e, i=i: e.memset(xp[i][:, 0:16], 0.0), writes=[f"xp{i}"])
            else:
                add("pool", lambda e, i=i, c=c: e.tensor_copy(out=xp[i][:, 0:16], in_=hp[:, c * 16:(c + 1) * 16]), reads=[f"hp{c}"], writes=[f"xp{i}"])
            add("act", lambda e, i=i, pa=pa, c=c: e.activation(out=xp[i][:, 16:E], in_=pa, func=AF.Identity, bias=sm("pb", 16 + c), scale=1.0),
                reads=[f"pj{i}a", "sm_pb"], writes=[f"xp{i}"], banks=[f"PJ{i}"])
            add("pool", lambda e, i=i, c=c: e.tensor_copy(out=hp[:, c * 16:(c + 1) * 16], in_=xp[i][:, T:E]), reads=[f"xp{i}"], writes=[f"hp{c}"])
            add("act", lambda e, pb_=pb_, c=c, j4=j4: e.activation(out=sgp[j4][:], in_=pb_, func=AF.Tanh, bias=sm("hpb", 24 + c), scale=0.5),
                reads=[f"pj{i}b", "sm_hpb"], writes=[f"sgp{j4}"], banks=[f"PJ{i}"])
            add("pool", lambda e, j4=j4: ts1(e, sgp[j4][:], sgp[j4][:], 1.0, ALU.add), reads=[f"sgp{j4}"], writes=[f"sgp{j4}"])
            add("dve", lambda e, pb_=pb_, c=c, j4=j4: e.scalar_tensor_tensor(out=sgp[j4][:], in0=pb_, scalar=sm("pb", 24 + c), in1=sgp[j4][:], op0=ALU.add, op1=ALU.mult),
                reads=[f"pj{i}b", "sm_pb", f"sgp{j4}"], writes=[f"sgp{j4}"], banks=[f"PJ{i}"])
            bufs = [xp[i], tmpA, tmpB]
            keys = [f"xp{i}", "tmpA", "tmpB"]
            cur = 0
            lo = 0
            for k in range(1, m + 1):
                sh = 1 << (k - 1)
                nxt = 1 if cur != 1 else 2
                lo2 = lo + sh
                add("pool", lambda e, cur=cur, nxt=nxt, lo2=lo2, sh=sh: e.tensor_tensor(out=bufs[nxt][:, lo2:E], in0=bufs[cur][:, lo2:E], in1=bufs[cur][:, lo2 - sh:E - sh], op=ALU.add),
                    reads=[keys[cur]], writes=[keys[nxt]])
                cur, lo = nxt, lo2
            w = float(1 << m)
            add("dve", lambda e, cur=cur, i=i, c=c, w=w: e.scalar_tensor_tensor(out=pl[c // 2 % 2][:, (c % 2) * T:(c % 2 + 1) * T], in0=bufs[cur][:, 16:E], scalar=1.0 / w,
                                                                              in1=xp[i][:, 16:E], op0=ALU.mult, op1=ALU.subtract),
                reads=[keys[cur], f"xp{i}"], writes=[f"pl{c // 2 % 2}_{c % 2}"])
            if t == 0:
                add("dve", lambda e, cur=cur, c=c: e.tensor_tensor(out=bufs[cur][:, 16:32], in0=bufs[cur][:, 16:32], in1=invc[:, c * 16:(c + 1) * 16], op=ALU.mult),
                    reads=[keys[cur], "invc", f"pl{c // 2 % 2}_{c % 2}"], writes=[keys[cur]])
                add("dve", lambda e, cur=cur, i=i, c=c: e.tensor_tensor(out=pl[c // 2 % 2][:, (c % 2) * T:(c % 2) * T + 16], in0=bufs[cur][:, 16:32], in1=xp[i][:, 16:32], op=ALU.subtract),
                    reads=[keys[cur], f"xp{i}"], writes=[f"pl{c // 2 % 2}_{c % 2}"])

        def pool_back(t, g):
            pi = g % 2

            def mm(e, g=g, pi=pi):
                r = []
                for mm_ in range(2):
                    for kk in range(2):
                        r.append(e.matmul(pq[:, mm_ * T:(mm_ + 1) * T], lhsT=pw_bf[:, (g * 2 + kk) * 256 + mm_ * 128:(g * 2 + kk) * 256 + (mm_ + 1) * 128],
                                          rhs=pl[pi][:, kk * T:(kk + 1) * T], start=(kk == 0), stop=(kk == 1)))
                return r
            add("pe", mm, reads=["pw_bf", f"pl{pi}_0", f"pl{pi}_1"], writes=["pq"], banks=["PQ"])
            for mm_ in range(2):
                c = 2 * g + mm_
                i = c % 4
                add("dve", lambda e, c=c, i=i, mm_=mm_: e.scalar_tensor_tensor(out=yT[:, (8 + c) * T:(9 + c) * T], in0=pq[:, mm_ * T:(mm_ + 1) * T], scalar=vcol(72, c), in1=sgp[i][:],
                                                                             op0=ALU.add, op1=ALU.mult),
                    reads=["pq", "vecs", f"sgp{i}"], writes=[f"yT{8 + c}"], banks=["PQ"])

        def out_stage(src, stag, dst, dtag, t):
            for s in range(NS):
                g = t * NS + s
                sl = g % 2
                row0 = t * T + s * 128
                add("sp", lambda e, sl=sl, row0=row0: e.dma_start(out=xo[sl][:], in_=src[row0:row0 + 128, :]), reads=[f"dram_{stag}_{g}"], writes=[f"xo{sl}"], dkey=f"xo{sl}")

                def mm(e, s=s):
                    r = []
                    for hh in range(2):
                        for cc in range(16):
                            r.append(e.matmul(po[:, hh * 512:(hh + 1) * 512], lhsT=yT[:, cc * T + s * 128: cc * T + (s + 1) * 128],
                                              rhs=w_out_bf[:, cc * 1024 + hh * 512: cc * 1024 + (hh + 1) * 512], start=(cc == 0), stop=(cc == 15)))
                    return r
                add("pe", mm, reads=["w_out_bf"] + [f"yT{cc}" for cc in range(16)], writes=["po"], banks=["PO0", "PO1"])
                add("act", lambda e: e.activation(out=tmpo[:], in_=po[:], func=AF.Square, accum_out=stat[:, 8:9]), reads=["po"], writes=["tmpo", "ssO"], banks=["PO0", "PO1"])
                add("pool", lambda e: e.tensor_scalar(out=stat[:, 9:10], in0=stat[:, 8:9], scalar1=1.0 / D, scalar2=EPS, op0=ALU.mult, op1=ALU.add), reads=["ssO"], writes=["vO"])
                add("pool", lambda e: e.tensor_tensor(out=stat[:, 10:11], in0=stat[:, 9:10], in1=mhalf[:, 0:1], op=ALU.pow), reads=["vO", "mhalf"], writes=["rsO"])
                add("dve", lambda e: e.scalar_tensor_tensor(out=tmpo[:], in0=po[:], scalar=stat[:, 10:11], in1=gg[:], op0=ALU.mult, op1=ALU.mult),
                    reads=["po", "rsO", "gg"], writes=["tmpo"], banks=["PO0", "PO1"])
                add("pool", lambda e, sl=sl: e.tensor_tensor(out=xo[sl][:], in0=xo[sl][:], in1=tmpo[:], op=ALU.add), reads=[f"xo{sl}", "tmpo"], writes=[f"xo{sl}"])
                add("sp", lambda e, sl=sl, row0=row0: e.dma_start(out=dst[row0:row0 + 128, :], in_=xo[sl][:]), reads=[f"xo{sl}"], writes=[f"dram_{dtag}_{g}"], dkey=f"xo{sl}")

        for li, l in enumerate(layers):
            src = x_in if li == 0 else mids[li - 1]
            dst = out_d if li == NL - 1 else mids[li]
            stag = "x" if li == 0 else f"mid{li - 1}"
            dtag = "out" if li == NL - 1 else f"mid{li}"
            prologue(l)
            stage_a(src, stag, 0)
            for t in range(nt):
                for step in range(8 + 1):
                    if step < 8:
                        rnn_front(t, step)
                    if step >= 1:
                        rnn_gates(t, step - 1)
                if t > 0:
                    out_stage(src, stag, dst, dtag, t - 1)
                for c in range(8):
                    pool_front(t, c)
                    if c % 2 == 1 and c >= 3:
                        pool_back(t, c // 2 - 1)
                pool_back(t, 3)
                if t + 1 < nt:
                    stage_a(src, stag, t + 1)
                for c in range(8):
                    rnn_back(t, c)
            out_stage(src, stag, dst, dtag, nt - 1)
            barrier(U_MAIN_KEYS, "U_main_done")

        S.emit(nc, block, sems, dsems, final_wait_keys=["xo0", "xo1"])
    return nc


def _host_layout(inp):
    f = np.float32
    L = DEPTH

    def pk(v):
        return np.ascontiguousarray(np.asarray(v, f).reshape(L, 8, 128).transpose(0, 2, 1))

    ada_w = np.asarray(inp["ada_w"], f).reshape(L, 2, 4, 128, 6, 512)
    ada_w = np.ascontiguousarray(ada_w.transpose(0, 4, 1, 3, 2, 5)).reshape(L * 6 * 2, 128, 2048)
    w_in = np.asarray(inp["w_in"], f).reshape(L, 2, 4, 128, 8, 512)
    w_in = np.ascontiguousarray(w_in.transpose(0, 4, 1, 3, 2, 5)).reshape(L * 8 * 2, 128, 2048)
    w_out = np.asarray(inp["w_out"], f).reshape(L, 4, 2, 2, 128, 1024)
    w_out = np.ascontiguousarray(w_out.transpose(0, 1, 2, 4, 3, 5)).reshape(L * 4 * 2, 128, 2048)
    ga_w = np.ascontiguousarray(np.asarray(inp["gate_a_w"], f).transpose(0, 2, 1, 3)).reshape(L, 128, 1024)
    gx_w = np.ascontiguousarray(np.asarray(inp["gate_x_w"], f).transpose(0, 2, 1, 3)).reshape(L, 128, 1024)
    pw = np.asarray(inp["pool_w"], f).reshape(L, 4, 2, 128, 256)
    pw = np.ascontiguousarray(pw.transpose(0, 3, 1, 2, 4)).reshape(L, 128, 2048)
    conv_w = np.asarray(inp["conv_w"], f).reshape(L, 4, 8, 128)
    conv_w = np.ascontiguousarray(conv_w.transpose(0, 3, 2, 1)).reshape(L, 128, 32)
    vecs = np.concatenate([pk(inp["pre_norm_g"]), conv_w, pk(inp["conv_b"]),
                           pk(np.asarray(inp["gate_a_b"], f).reshape(L, 1024)), pk(np.asarray(inp["gate_x_b"], f).reshape(L, 1024)),
                           pk(inp["lru_lambda"]), pk(np.asarray(inp["pool_b"], f).reshape(L, 1024)), pk(inp["pool_scale"])], axis=2)
    assert vecs.shape == (L, 128, NV)
    ada_b_bc = np.ascontiguousarray(np.broadcast_to(np.asarray(inp["ada_b"], f)[:, None, :], (L, 128, 3 * D)))
    post_g_bc = np.ascontiguousarray(np.broadcast_to(np.asarray(inp["post_norm_g"], f)[:, None, :], (L, 128, D)))
    invcnt = np.zeros((128, 8, 16), f)
    for c in range(8):
        w = 2 << (c // 2)
        invcnt[:, c, :] = 1.0 / np.minimum(np.arange(16) + 1, w)
    shared = {"ada_w": ada_w, "ada_b_bc": ada_b_bc, "post_g_bc": post_g_bc, "w_in": w_in, "w_out": w_out, "ga_w": ga_w, "gx_w": gx_w,
              "pool_w": pw, "vecs": np.ascontiguousarray(vecs), "ident": np.eye(128, dtype=f), "invcnt": invcnt.reshape(128, 128)}
    c = np.asarray(inp["c"], f)
    cbcs = []
    for b in range(c.shape[0]):
        cb = c[b].reshape(8, 128).T
        cbcs.append(np.ascontiguousarray(np.broadcast_to(cb[:, :, None], (128, 8, 128))).reshape(128, 1024))
    return shared, cbcs


LAUNCH_GROUPS = [[0, 1]]


def kernel(**inputs):
    x = np.asarray(inputs["x"], np.float32)
    nb = x.shape[0]
    shared, cbcs = _host_layout(inputs)
    cur = [np.ascontiguousarray(x[b]) for b in range(nb)]
    for grp in LAUNCH_GROUPS:
        nc = build_nc(list(grp))
        in_maps = [dict(shared, x=cur[b], cbc=cbcs[b]) for b in range(nb)]
        res = run_bass_kernel_spmd(nc, in_maps, core_ids=list(range(nb)))
        cur = [np.asarray(res.results[b]["out"], np.float32) for b in range(nb)]
    return np.stack(cur, axis=0)
```

```python
import numpy as np
import concourse.bass as bass
import concourse.mybir as mybir
from concourse.alu_op_type import AluOpType as ALU
from concourse.bass_utils import run_bass_kernel_spmd

F32 = mybir.dt.float32
BF16 = mybir.dt.bfloat16
AF = mybir.ActivationFunctionType

ENGS = ("pe", "act", "dve", "pool", "sp")


class _Op:
    __slots__ = ("eng", "fn", "reads", "writes", "deps", "signal", "sem", "val", "dkey", "ndma", "banks")

    def __init__(self, eng, fn, reads, writes, dkey, ndma, banks=()):
        self.eng, self.fn, self.reads, self.writes = eng, fn, tuple(reads), tuple(writes)
        self.banks = tuple(banks)
        self.deps = []
        self.signal = False
        self.sem = None
        self.val = 0
        self.dkey = dkey
        self.ndma = ndma


class Sched:
    def __init__(self, same_eng_wait=True):
        self.ops = []
        self.same_eng_wait = same_eng_wait

    def add(self, eng, fn, reads=(), writes=(), dkey=None, ndma=1, banks=()):
        assert eng in ENGS
        if eng == "sp" and dkey is None:
            dkey = writes[0] if writes else reads[0]
        self.ops.append(_Op(eng, fn, reads, writes, dkey, ndma, banks))

    def analyze(self):
        last_w = {}
        readers = {}
        bank_last = {}
        pos_in_eng = {}
        cnt = {e: 0 for e in ENGS}
        for i, op in enumerate(self.ops):
            pos_in_eng[i] = cnt[op.eng]
            cnt[op.eng] += 1
            deps = set()
            for r in op.reads:
                if r in last_w:
                    deps.add(last_w[r])
            for w in op.writes:
                if w in last_w:
                    deps.add(last_w[w])
                for j in readers.get(w, ()):
                    deps.add(j)
            for b in op.banks:
                bl = bank_last.setdefault(b, {})
                for e2, j in bl.items():
                    if e2 != op.eng:
                        deps.add(j)
                bl[op.eng] = i
            deps.discard(i)
            best = {}
            keep = []
            for j in deps:
                e = self.ops[j].eng
                if e == "sp":
                    keep.append(j)
                    continue
                if e == op.eng:
                    if e == "pe":
                        continue
                    if not self.same_eng_wait:
                        continue
                if e not in best or j > best[e]:
                    best[e] = j
            keep.extend(best.values())
            op.deps = sorted(keep)
            for j in op.deps:
                self.ops[j].signal = True
            for w in op.writes:
                last_w[w] = i
                readers[w] = []
            for r in op.reads:
                readers.setdefault(r, []).append(i)

    def emit(self, nc, block, sems, dma_sems, final_wait_keys=()):
        self.analyze()
        cnt = {e: 0 for e in ENGS}
        dcnt = {}
        for op in self.ops:
            if op.eng == "sp":
                dcnt[op.dkey] = dcnt.get(op.dkey, 0) + 16 * op.ndma
                op.sem, op.val = dma_sems[op.dkey], dcnt[op.dkey]
            elif op.signal:
                cnt[op.eng] += 1
                op.sem, op.val = sems[op.eng], cnt[op.eng]
        ops = self.ops

        def run(eng_name, e):
            waited = {}
            for op in ops:
                if op.eng != eng_name:
                    continue
                for j in op.deps:
                    d = ops[j]
                    key = id(d.sem)
                    if waited.get(key, 0) >= d.val:
                        continue
                    e.wait_ge(d.sem, d.val)
                    waited[key] = d.val
                res = op.fn(e)
                if op.eng == "sp":
                    lst = res if isinstance(res, (list, tuple)) else [res]
                    assert len(lst) == op.ndma, (len(lst), op.ndma)
                    for ins in lst:
                        ins.then_inc(op.sem, 16)
                elif op.signal:
                    ins = res[-1] if isinstance(res, (list, tuple)) else res
                    ins.then_inc(op.sem, 1)
            if eng_name == "sp":
                for k in final_wait_keys:
                    if k in dcnt:
                        e.wait_ge(dma_sems[k], dcnt[k])

        @block.tensor
        def _(e):
            run("pe", e)

        @block.scalar
        def _(e):
            run("act", e)

        @block.vector
        def _(e):
            run("dve", e)

        @block.gpsimd
        def _(e):
            run("pool", e)

        @block.sync
        def _(e):
            run("sp", e)


D = 1024
SEQ = 4096
NB = 8
DEPTH = 2
T = 256
NS = T // 128
NT = SEQ // T
EPS = 1e-6
KC = 8
SAME_ENG_WAIT = True
POOL_ADDS_ON_POOL = 2
OUT_HOOK = (5, 7)
STAGEA_HOOK = (9, 10)
NV = 88
LOG1P_COEF = [1.0 / (2 * k + 1) for k in range(8)]


def build_nc(layers, seq=SEQ):
    from contextlib import ExitStack
    nt = seq // T
    nc = bass.Bass("TRN2", target_bir_lowering=False)
    NL = len(layers)

    def din(name, shape, dt=F32):
        return nc.dram_tensor(name, shape, dt, kind="ExternalInput").ap()

    x_in = din("x", [seq, D])
    cbc_d = din("cbc", [128, KC * 128])
    ada_w_d = din("ada_w", [DEPTH * 6 * 2, 128, 2048])
    ada_b_d = din("ada_b_bc", [DEPTH, 128, 3 * D])
    postg_d = din("post_g_bc", [DEPTH, 128, D])
    w_in_d = din("w_in", [DEPTH * 8 * 2, 128, 2048])
    w_out_d = din("w_out", [DEPTH * 4 * 2, 128, 2048])
    ga_w_d = din("ga_w", [DEPTH, 128, 1024])
    gx_w_d = din("gx_w", [DEPTH, 128, 1024])
    pw_d = din("pool_w", [DEPTH, 128, 2048])
    vecs_d = din("vecs", [DEPTH, 128, NV])
    ident_d = din("ident", [128, 128])
    invc_d = din("invcnt", [128, 8 * 16])
    out_d = nc.dram_tensor("out", [seq, D], F32, kind="ExternalOutput").ap()
    mids = [nc.dram_tensor(f"xmid{i}", [seq, D], F32, kind="Internal").ap() for i in range(NL - 1)]

    with ExitStack() as es:
        def sb(name, shape, dt=F32):
            return es.enter_context(nc.sbuf_tensor(name, shape, dt))

        def ps(name, shape, dt=F32):
            return es.enter_context(nc.psum_tensor(name, shape, dt))

        w_in_bf = sb("w_in_bf", [128, KC * 4096], BF16)
        w_out_bf = sb("w_out_bf", [128, 16 * 1024], BF16)
        ga_bf = sb("ga_bf", [128, 1024], BF16)
        gx_bf = sb("gx_bf", [128, 1024], BF16)
        pw_bf = sb("pw_bf", [128, 2048], BF16)
        U = sb("U", [128, 8192], F32)
        xs = [sb(f"xs{i}", [128, D]) for i in range(2)]
        xn = [sb(f"xn{i}", [128, D], BF16) for i in range(2)]
        hT = [sb(f"hT{i}", [128, KC * T], BF16) for i in range(2)]
        xr = [sb(f"xr{i}", [128, 4 + T]) for i in range(2)]
        ub = [sb(f"u{i}", [128, T]) for i in range(3)]
        ubf = [sb(f"ubf{i}", [128, T], BF16) for i in range(2)]
        hb = [sb(f"h{i}", [128, T]) for i in range(2)]
        xp = [sb(f"xp{i}", [128, 16 + T]) for i in range(2)]
        tmpA = sb("tmpA", [128, 16 + T])
        tmpB = sb("tmpB", [128, 16 + T])
        pl = [sb(f"pl{i}", [128, 2 * T], BF16) for i in range(2)]
        sgp = [sb(f"sgp{i}", [128, T]) for i in range(4)]
        gsb = [sb(f"gsb{i}", [128, T]) for i in range(3)]
        yT = sb("yT", [128, 16 * T], BF16)
        xo = [sb(f"xo{i}", [128, D]) for i in range(2)]
        tmpo = sb("tmpo", [128, D])
        gg = sb("gg", [128, D])
        vecs = sb("vecs_sb", [128, NV])
        identf = sb("identf", [128, 128])
        identb = sb("identb", [128, 128], BF16)
        invc = sb("invc", [128, 128])
        mhalf = sb("mhalf", [128, 8])
        small = sb("small", [128, 256])
        stat = sb("stat", [128, 64])
        hx = sb("hx", [128, 8 * 4])
        hp = sb("hp", [128, 8 * 16])
        hstate = sb("hstate", [128, 8])

        def Uv(off, n):
            return U[:, off:off + n]
        A = [Uv(c * T, T) for c in range(8)]
        A2 = [Uv(2048 + c * T, T) for c in range(8)]
        T1 = [Uv(4096 + c * T, T) for c in range(8)]
        SG = [Uv(6144 + c * T, T) for c in range(8)]
        stage = [Uv(0, 2048), Uv(2048, 2048)]
        mod_bc = Uv(4096, 3072)
        cact = Uv(7168, 1024)

        SM = {}
        _o = [0]

        def smalloc(name, n):
            SM[name] = (_o[0], n)
            _o[0] += n
        for nm in ("shift", "scale", "gs", "nsp4", "nsp8", "hga", "hgx"):
            smalloc(nm, 8)
        smalloc("ws", 16)
        smalloc("pb", 32)
        smalloc("hpb", 32)
        for nm in ("sp_e", "sp_z", "sp_z2", "sp_p", "sp_t", "sp_ax", "sp_mx"):
            smalloc(nm, 8)
        assert _o[0] <= 256

        def sm(name, c=None, n=1):
            o, w = SM[name]
            if c is None:
                return small[:, o:o + w]
            return small[:, o + c:o + c + n]

        def vcol(o, c=None, n=8):
            if c is None:
                return vecs[:, o:o + n]
            return vecs[:, o + c:o + c + 1]

        pj = [ps(f"pj{i}", [128, 2 * T]) for i in range(2)]
        pz = [ps(f"pz{i}", [128, 2 * T]) for i in range(2)]
        pq = ps("pq", [128, 2 * T])
        ptr = ps("ptr", [128, KC * 128], BF16)
        po = ps("po", [128, D])

        sems = {e: es.enter_context(nc.semaphore("s_" + e)) for e in ENGS if e != "sp"}
        dkeys = ["xs0", "xs1", "xo0", "xo1", "stage0", "stage1", "identf", "invc", "cact", "vecs", "gg"] + [f"mod_bc{i}" for i in range(6)]
        dsems = {k: es.enter_context(nc.semaphore("d_" + k)) for k in dkeys}
        block = es.enter_context(nc.Block())
        S = Sched(same_eng_wait=SAME_ENG_WAIT)
        add = S.add

        def ts1(e, out, in0, s1, op0=ALU.mult):
            if op0 == ALU.mult:
                return e.tensor_scalar(out=out, in0=in0, scalar1=s1, scalar2=0.0, op0=ALU.mult, op1=ALU.add)
            assert op0 == ALU.add
            return e.tensor_scalar(out=out, in0=in0, scalar1=s1, scalar2=1.0, op0=ALU.add, op1=ALU.mult)

        add("sp", lambda e: e.dma_start(out=identf[:], in_=ident_d), writes=["identf"])
        add("sp", lambda e: e.dma_start(out=invc[:], in_=invc_d), writes=["invc"])
        add("dve", lambda e: e.tensor_copy(out=identb[:], in_=identf[:]), reads=["identf"], writes=["identb"])
        add("pool", lambda e: e.memset(mhalf[:], -0.5), writes=["mhalf"])
        cnt = {"stage": 0, "cast": 0}

        def barrier(keys, token):
            add("pool", lambda e: e.memset(stat[:, 63:64], 0.0), writes=list(keys) + [token, "stat63"])

        U_MAIN_KEYS = [f"A{c}" for c in range(8)] + [f"A2{c}" for c in range(8)] + \
                      [f"T1{c}" for c in range(8)] + [f"SG{c}" for c in range(8)]
        MODK = [f"mod_bc{i}" for i in range(6)]
        U_PRO_KEYS = ["stage0", "stage1", "cact"] + MODK

        def stage_load(src_ap, width=2048):
            i = cnt["stage"] % 2
            cnt["stage"] += 1
            add("sp", lambda e: e.dma_start(out=stage[i][:, 0:width], in_=src_ap), reads=["U_main_done"], writes=[f"stage{i}"], dkey=f"stage{i}")
            return i

        def scaled_cast(dst, src, scal, skey, reads, writes):
            cnt["cast"] += 1
            eng = ("act", "dve", "pool")[cnt["cast"] % 3]
            if eng == "act":
                add("act", lambda e: e.activation(out=dst, in_=src, func=AF.Copy, scale=scal), reads=reads + [skey], writes=writes)
            else:
                add(eng, lambda e: ts1(e, dst, src, scal), reads=reads + [skey], writes=writes)

        def prologue(l):
            add("sp", lambda e: e.dma_start(out=cact, in_=cbc_d), reads=["U_main_done"], writes=["cact"])
            add("act", lambda e: e.activation(out=mod_bc[:, 0:1024], in_=cact, func=AF.Tanh, scale=0.5), reads=["cact", "U_main_done"], writes=["mod_bc0", "mod_bc1"])
            add("dve", lambda e: e.scalar_tensor_tensor(out=cact, in0=mod_bc[:, 0:1024], scalar=1.0, in1=cact, op0=ALU.add, op1=ALU.mult),
                reads=["mod_bc0", "mod_bc1", "cact"], writes=["cact"])
            add("dve", lambda e: ts1(e, cact, cact, 0.5), reads=["cact"], writes=["cact"])
            add("sp", lambda e: e.dma_start(out=vecs[:], in_=vecs_d[l]), writes=["vecs"])
            for nb in range(6):
                for kh in range(2):
                    si = stage_load(ada_w_d[(l * 6 + nb) * 2 + kh])

                    def mm(e, si=si, kh=kh):
                        r = []
                        for k4 in range(4):
                            kc = kh * 4 + k4
                            r.append(e.matmul(po[:, 0:512], lhsT=cact[:, kc * 128:(kc + 1) * 128], rhs=stage[si][:, k4 * 512:(k4 + 1) * 512],
                                              start=(kc == 0), stop=(kc == 7)))
                        return r
                    add("pe", mm, reads=[f"stage{si}", "cact"], writes=["po"], banks=["PO0"])
                add("sp", lambda e, nb=nb: e.dma_start(out=mod_bc[:, nb * 512:(nb + 1) * 512], in_=ada_b_d[l][:, nb * 512:(nb + 1) * 512]),
                    reads=["U_main_done"], writes=[f"mod_bc{nb}"])
                add("dve", lambda e, nb=nb: e.tensor_tensor(out=mod_bc[:, nb * 512:(nb + 1) * 512], in0=po[:, 0:512], in1=mod_bc[:, nb * 512:(nb + 1) * 512], op=ALU.add),
                    reads=["po", f"mod_bc{nb}"], writes=[f"mod_bc{nb}"], banks=["PO0"])
            for which, base in (("shift", 0), ("scale", 1024)):
                for kc in range(8):
                    nbk = (base + kc * 128) // 512
                    add("dve", lambda e, which=which, base=base, kc=kc: e.scalar_tensor_tensor(
                        out=tmpA[:, 0:128], in0=mod_bc[:, base + kc * 128:base + (kc + 1) * 128], scalar=1.0, in1=identf[:],
                        op0=ALU.mult, op1=ALU.mult, accum_out=sm(which, kc)),
                        reads=[f"mod_bc{nbk}", "identf"], writes=["tmpA", f"sm_{which}"])
            add("dve", lambda e: e.scalar_tensor_tensor(out=sm("gs"), in0=sm("scale"), scalar=1.0, in1=vcol(0), op0=ALU.add, op1=ALU.mult),
                reads=["sm_scale", "vecs"], writes=["sm_gs"])
            add("sp", lambda e: e.dma_start(out=gg[:], in_=postg_d[l]), writes=["gg"])
            for hh in range(2):
                add("pool", lambda e, hh=hh: e.tensor_tensor(out=gg[:, hh * 512:(hh + 1) * 512], in0=gg[:, hh * 512:(hh + 1) * 512],
                                                            in1=mod_bc[:, 2048 + hh * 512:2048 + (hh + 1) * 512], op=ALU.mult),
                    reads=["gg", f"mod_bc{4 + hh}"], writes=["gg"])
            lam = vcol(64)
            add("dve", lambda e: e.scalar_tensor_tensor(out=sm("sp_ax"), in0=lam, scalar=-1.0, in1=lam, op0=ALU.mult, op1=ALU.max), reads=["vecs"], writes=["sm_sp_ax"])
            add("act", lambda e: e.activation(out=sm("sp_e"), in_=sm("sp_ax"), func=AF.Exp, scale=-1.0), reads=["sm_sp_ax"], writes=["sm_sp_e"])
            add("dve", lambda e: ts1(e, sm("sp_t"), sm("sp_e"), 2.0, ALU.add), reads=["sm_sp_e"], writes=["sm_sp_t"])
            add("dve", lambda e: e.reciprocal(out=sm("sp_t"), in_=sm("sp_t")), reads=["sm_sp_t"], writes=["sm_sp_t"])
            add("dve", lambda e: e.tensor_tensor(out=sm("sp_z"), in0=sm("sp_e"), in1=sm("sp_t"), op=ALU.mult), reads=["sm_sp_e", "sm_sp_t"], writes=["sm_sp_z"])
            add("dve", lambda e: e.tensor_tensor(out=sm("sp_z2"), in0=sm("sp_z"), in1=sm("sp_z"), op=ALU.mult), reads=["sm_sp_z"], writes=["sm_sp_z2"])
            add("dve", lambda e: e.tensor_scalar(out=sm("sp_p"), in0=sm("sp_z2"), scalar1=LOG1P_COEF[7], scalar2=LOG1P_COEF[6], op0=ALU.mult, op1=ALU.add),
                reads=["sm_sp_z2"], writes=["sm_sp_p"])
            for k in (5, 4, 3, 2, 1, 0):
                add("dve", lambda e: e.tensor_tensor(out=sm("sp_p"), in0=sm("sp_p"), in1=sm("sp_z2"), op=ALU.mult), reads=["sm_sp_p", "sm_sp_z2"], writes=["sm_sp_p"])
                add("dve", lambda e, k=k: ts1(e, sm("sp_p"), sm("sp_p"), LOG1P_COEF[k], ALU.add), reads=["sm_sp_p"], writes=["sm_sp_p"])
            add("dve", lambda e: e.tensor_tensor(out=sm("sp_p"), in0=sm("sp_p"), in1=sm("sp_z"), op=ALU.mult), reads=["sm_sp_p", "sm_sp_z"], writes=["sm_sp_p"])
            add("dve", lambda e: e.tensor_scalar(out=sm("sp_mx"), in0=lam, scalar1=-1.0, scalar2=0.0, op0=ALU.mult, op1=ALU.max), reads=["vecs"], writes=["sm_sp_mx"])
            add("dve", lambda e: e.scalar_tensor_tensor(out=sm("sp_t"), in0=sm("sp_p"), scalar=2.0, in1=sm("sp_mx"), op0=ALU.mult, op1=ALU.add),
                reads=["sm_sp_p", "sm_sp_mx", "sm_sp_t"], writes=["sm_sp_t"])
            add("dve", lambda e: ts1(e, sm("nsp4"), sm("sp_t"), -4.0), reads=["sm_sp_t"], writes=["sm_nsp4"])
            add("dve", lambda e: ts1(e, sm("nsp8"), sm("sp_t"), -8.0), reads=["sm_sp_t"], writes=["sm_nsp8"])
            add("dve", lambda e: ts1(e, sm("hga"), vcol(48), 0.5), reads=["vecs"], writes=["sm_hga"])
            add("dve", lambda e: ts1(e, sm("hgx"), vcol(56), 0.5), reads=["vecs"], writes=["sm_hgx"])
            add("pool", lambda e: e.memset(sm("ws", 0, 8), 0.25), writes=["sm_ws"])
            add("dve", lambda e: ts1(e, sm("ws", 8, 8), vcol(80), 0.5), reads=["vecs", "sm_ws"], writes=["sm_ws"])
            add("pool", lambda e: e.memset(xo[0][:, 0:128], 1.0), writes=["xo0"])
            for kc in range(8):
                add("dve", lambda e, kc=kc: ts1(e, tmpo[:, kc * 128:(kc + 1) * 128], xo[0][:, 0:128], sm("shift", kc)), reads=["xo0", "sm_shift"], writes=["tmpo"])
            for cb in range(8):
                for kh in range(2):
                    si = stage_load(w_in_d[(l * 8 + cb) * 2 + kh])
                    for k4 in range(4):
                        kc = kh * 4 + k4
                        dst = w_in_bf[:, kc * 4096 + cb * 512: kc * 4096 + (cb + 1) * 512]
                        src = stage[si][:, k4 * 512:(k4 + 1) * 512]
                        scaled_cast(dst, src, sm("gs", kc), "sm_gs", [f"stage{si}"], [f"w_in_bf{cb}"])

                    def mmb(e, si=si, kh=kh):
                        r = []
                        for k4 in range(4):
                            kc = kh * 4 + k4
                            r.append(e.matmul(po[:, 512:1024], lhsT=tmpo[:, kc * 128:(kc + 1) * 128], rhs=stage[si][:, k4 * 512:(k4 + 1) * 512],
                                              start=(kc == 0), stop=(kc == 7)))
                        return r
                    add("pe", mmb, reads=[f"stage{si}", "tmpo"], writes=["po_b"], banks=["PO1"])
                for jb in range(4):
                    add("dve", lambda e, cb=cb, jb=jb: e.scalar_tensor_tensor(
                        out=tmpA[:, 0:128], in0=po[:, 512 + jb * 128:512 + (jb + 1) * 128], scalar=1.0, in1=identf[:],
                        op0=ALU.mult, op1=ALU.mult, accum_out=sm("pb", cb * 4 + jb)),
                        reads=["po_b", "po", "identf"], writes=["tmpA", "sm_pb"], banks=["PO1"])
            add("dve", lambda e: ts1(e, sm("hpb"), sm("pb"), 0.5), reads=["sm_pb"], writes=["sm_hpb"])
            for rb in range(4):
                for hh in range(2):
                    si = stage_load(w_out_d[(l * 4 + rb) * 2 + hh])
                    for c2 in range(2):
                        cc = rb * 4 + hh * 2 + c2
                        scaled_cast(w_out_bf[:, cc * 1024:(cc + 1) * 1024], stage[si][:, c2 * 1024:(c2 + 1) * 1024], sm("ws", cc), "sm_ws",
                                    [f"stage{si}"], ["w_out_bf"])
            si = stage_load(ga_w_d[l], 1024)
            add("pool", lambda e, si=si: e.tensor_copy(out=ga_bf[:], in_=stage[si][:, 0:1024]), reads=[f"stage{si}"], writes=["ga_bf"])
            si = stage_load(gx_w_d[l], 1024)
            add("pool", lambda e, si=si: e.tensor_copy(out=gx_bf[:], in_=stage[si][:, 0:1024]), reads=[f"stage{si}"], writes=["gx_bf"])
            si = stage_load(pw_d[l], 2048)
            add("dve", lambda e, si=si: e.tensor_copy(out=pw_bf[:], in_=stage[si][:, 0:2048]), reads=[f"stage{si}"], writes=["pw_bf"])
            barrier(U_PRO_KEYS, "U_pro_done")

        def stage_a_sub(src, stag, t, s):
            hs = t % 2
            g = t * NS + s
            sl = g % 2
            row0 = t * T + s * 128
            add("sp", lambda e: e.dma_start(out=xs[sl][:], in_=src[row0:row0 + 128, :]), reads=[f"dram_{stag}_{g}"], writes=[f"xs{sl}"], dkey=f"xs{sl}")
            add("act", lambda e: e.activation(out=xn[sl][:], in_=xs[sl][:], func=AF.Square, accum_out=stat[:, sl:sl + 1]),
                reads=[f"xs{sl}"], writes=[f"xn{sl}", f"ssA{sl}"])
            add("pool", lambda e: e.tensor_scalar(out=stat[:, 2 + sl:3 + sl], in0=stat[:, sl:sl + 1], scalar1=1.0 / D, scalar2=EPS, op0=ALU.mult, op1=ALU.add),
                reads=[f"ssA{sl}"], writes=[f"vA{sl}"])
            add("pool", lambda e: e.tensor_tensor(out=stat[:, 4 + sl:5 + sl], in0=stat[:, 2 + sl:3 + sl], in1=mhalf[:, 0:1], op=ALU.pow),
                reads=[f"vA{sl}", "mhalf"], writes=[f"rsA{sl}"])
            add("act", lambda e: e.activation(out=xn[sl][:], in_=xs[sl][:], func=AF.Copy, scale=stat[:, 4 + sl:5 + sl]),
                reads=[f"xs{sl}", f"rsA{sl}"], writes=[f"xn{sl}"])

            def tr(e):
                return [e.transpose(out=ptr[:, kc * 128:(kc + 1) * 128], in_=xn[sl][:, kc * 128:(kc + 1) * 128], identity=identb[:]) for kc in range(KC)]
            add("pe", tr, reads=[f"xn{sl}", "identb"], writes=["ptr"], banks=["PTR"])
            add("act", lambda e: e.activation(
                out=hT[hs][:, :].rearrange("p (k t) -> p k t", k=KC)[:, :, s * 128:(s + 1) * 128],
                in_=ptr[:, :].rearrange("p (k t) -> p k t", k=KC), func=AF.Copy),
                reads=["ptr"], writes=[f"hT{hs}_{s}"], banks=["PTR"])

        def w_in_pair(t, i, q1, q2):
            hs = t % 2

            def mm(e):
                r = []
                for half, q in ((0, q1), (1, q2)):
                    for kc in range(KC):
                        r.append(e.matmul(pj[i][:, half * T:(half + 1) * T], lhsT=w_in_bf[:, kc * 4096 + q * 128: kc * 4096 + (q + 1) * 128],
                                          rhs=hT[hs][:, kc * T:(kc + 1) * T], start=(kc == 0), stop=(kc == KC - 1)))
                return r
            add("pe", mm, reads=[f"w_in_bf{q1 // 4}", f"w_in_bf{q2 // 4}"] + [f"hT{hs}_{s}" for s in range(NS)], writes=[f"pj{i}a", f"pj{i}b"], banks=[f"PJ{i}"])

        def R0(t, c):
            w_in_pair(t, c % 2, c, 8 + c)

        def R1a(t, c):
            i = c % 2
            gsl = c % 3
            pa, pb_ = pj[i][:, 0:T], pj[i][:, T:2 * T]
            add("act", lambda e: e.activation(out=SG[c], in_=pb_, func=AF.Tanh, bias=sm("hpb", 8 + c), scale=0.5),
                reads=[f"pj{i}b", "sm_hpb", "U_pro_done"], writes=[f"SG{c}"], banks=[f"PJ{i}"])
            add("act", lambda e: e.activation(out=gsb[gsl][:], in_=pb_, func=AF.Identity, bias=sm("pb", 8 + c), scale=1.0),
                reads=[f"pj{i}b", "sm_pb"], writes=[f"gsb{gsl}"], banks=[f"PJ{i}"])
            if t == 0:
                add("pool", lambda e: e.memset(xr[i][:, 0:3], 0.0), writes=[f"xr{i}h"])
            else:
                add("pool", lambda e: e.tensor_copy(out=xr[i][:, 0:3], in_=hx[:, c * 4:c * 4 + 3]), reads=[f"hx{c}"], writes=[f"xr{i}h"])
            add("act", lambda e: e.activation(out=xr[i][:, 3:3 + T], in_=pa, func=AF.Identity, bias=sm("pb", c), scale=1.0),
                reads=[f"pj{i}a", "sm_pb"], writes=[f"xr{i}"], banks=[f"PJ{i}"])
            add("pool", lambda e: e.tensor_copy(out=hx[:, c * 4:c * 4 + 3], in_=xr[i][:, T:T + 3]), reads=[f"xr{i}"], writes=[f"hx{c}"])

        def R1b(t, c):
            gsl = c % 3
            add("dve", lambda e: e.scalar_tensor_tensor(out=SG[c], in0=SG[c], scalar=1.0, in1=gsb[gsl][:], op0=ALU.add, op1=ALU.mult),
                reads=[f"gsb{gsl}", f"SG{c}"], writes=[f"SG{c}"])

        def R2(t, c):
            i = c % 2
            j = c % 3
            add("dve", lambda e: e.tensor_scalar(out=ub[j][:], in0=xr[i][:, 0:T], scalar1=vcol(8, c * 4 + 0), scalar2=vcol(40, c), op0=ALU.mult, op1=ALU.add),
                reads=[f"xr{i}", f"xr{i}h", "vecs"], writes=[f"u{j}"])
            for k in (1, 2, 3):
                add("dve", lambda e, k=k: e.scalar_tensor_tensor(out=ub[j][:], in0=xr[i][:, k:k + T], scalar=vcol(8, c * 4 + k), in1=ub[j][:],
                                                                 op0=ALU.mult, op1=ALU.add),
                    reads=[f"xr{i}", f"xr{i}h", "vecs", f"u{j}"], writes=[f"u{j}"])
            add("pool", lambda e: e.tensor_copy(out=ubf[i][:], in_=ub[j][:]), reads=[f"u{j}"], writes=[f"ubf{i}"])

        def R3(t, c):
            i = c % 2

            def mm(e):
                return [e.matmul(pz[i][:, 0:T], lhsT=ga_bf[:, c * 128:(c + 1) * 128], rhs=ubf[i][:], start=True, stop=True),
                        e.matmul(pz[i][:, T:2 * T], lhsT=gx_bf[:, c * 128:(c + 1) * 128], rhs=ubf[i][:], start=True, stop=True)]
            add("pe", mm, reads=["ga_bf", "gx_bf", f"ubf{i}"], writes=[f"pz{i}"], banks=[f"PZ{i}"])

        def R4(t, c):
            i = c % 2
            j = c % 3
            add("act", lambda e: e.activation(out=A[c], in_=pz[i][:, 0:T], func=AF.Tanh, bias=sm("hga", c), scale=0.5),
                reads=[f"pz{i}", "sm_hga", "U_pro_done"], writes=[f"A{c}"], banks=[f"PZ{i}"])
            add("act", lambda e: e.activation(out=T1[c], in_=pz[i][:, T:2 * T], func=AF.Tanh, bias=sm("hgx", c), scale=0.5),
                reads=[f"pz{i}", "sm_hgx", "U_pro_done"], writes=[f"T1{c}"], banks=[f"PZ{i}"])
            add("act", lambda e: e.activation(out=A2[c], in_=A[c], func=AF.Exp, bias=sm("nsp8", c), scale=sm("nsp8", c)),
                reads=[f"A{c}", "sm_nsp8", "U_pro_done"], writes=[f"A2{c}"])
            add("act", lambda e: e.activation(out=A[c], in_=A[c], func=AF.Exp, bias=sm("nsp4", c), scale=sm("nsp4", c)),
                reads=[f"A{c}", "sm_nsp4"], writes=[f"A{c}"])
            add("dve", lambda e: e.scalar_tensor_tensor(out=T1[c], in0=T1[c], scalar=1.0, in1=ub[j][:], op0=ALU.add, op1=ALU.mult),
                reads=[f"T1{c}", f"u{j}"], writes=[f"T1{c}"])

        def back_act(t, c):
            add("act", lambda e: e.activation(out=A2[c], in_=A2[c], func=AF.Sqrt, bias=1.0, scale=-1.0), reads=[f"A2{c}"], writes=[f"A2{c}"])

        def back_pair(t, c0):
            cs = (c0, c0 + 1)
            for c in cs:
                add("pool", lambda e, c=c: e.tensor_tensor(out=T1[c], in0=T1[c], in1=A2[c], op=ALU.mult), reads=[f"T1{c}", f"A2{c}"], writes=[f"T1{c}"])
            for c in cs:
                i = c % 2
                if t == 0:
                    add("dve", lambda e, c=c, i=i: e.tensor_tensor_scan(out=hb[i][:], data0=A[c], data1=T1[c], initial=0.0, op0=ALU.mult, op1=ALU.add),
                        reads=[f"A{c}", f"T1{c}"], writes=[f"h{i}"])
                else:
                    add("dve", lambda e, c=c, i=i: e.tensor_tensor_scan(out=hb[i][:], data0=A[c], data1=T1[c], initial=hstate[:, c:c + 1], op0=ALU.mult, op1=ALU.add),
                        reads=[f"A{c}", f"T1{c}", f"hst{c}"], writes=[f"h{i}"])
            for c in cs:
                i = c % 2
                add("pool", lambda e, c=c, i=i: e.tensor_copy(out=hstate[:, c:c + 1], in_=hb[i][:, T - 1:T]), reads=[f"h{i}"], writes=[f"hst{c}"])
                add("pool", lambda e, c=c, i=i: e.tensor_tensor(out=yT[:, c * T:(c + 1) * T], in0=hb[i][:], in1=SG[c], op=ALU.mult),
                    reads=[f"h{i}", f"SG{c}"], writes=[f"yT{c}"])

        E = 16 + T

        def P0(t, c):
            w_in_pair(t, c % 2, 16 + c, 24 + c)

        def P1a(t, c):
            i = c % 2
            j4 = c % 4
            gsl = (8 + c) % 3
            pa, pb_ = pj[i][:, 0:T], pj[i][:, T:2 * T]
            add("act", lambda e: e.activation(out=sgp[j4][:], in_=pb_, func=AF.Tanh, bias=sm("hpb", 24 + c), scale=0.5),
                reads=[f"pj{i}b", "sm_hpb"], writes=[f"sgp{j4}"], banks=[f"PJ{i}"])
            add("act", lambda e: e.activation(out=gsb[gsl][:], in_=pb_, func=AF.Identity, bias=sm("pb", 24 + c), scale=1.0),
                reads=[f"pj{i}b", "sm_pb"], writes=[f"gsb{gsl}"], banks=[f"PJ{i}"])
            if t == 0:
                add("pool", lambda e: e.memset(xp[i][:, 0:16], 0.0), writes=[f"xp{i}h"])
            else:
                add("pool", lambda e: e.tensor_copy(out=xp[i][:, 0:16], in_=hp[:, c * 16:(c + 1) * 16]), reads=[f"hp{c}"], writes=[f"xp{i}h"])
            add("act", lambda e: e.activation(out=xp[i][:, 16:E], in_=pa, func=AF.Identity, bias=sm("pb", 16 + c), scale=1.0),
                reads=[f"pj{i}a", "sm_pb"], writes=[f"xp{i}"], banks=[f"PJ{i}"])
            add("pool", lambda e: e.tensor_copy(out=hp[:, c * 16:(c + 1) * 16], in_=xp[i][:, T:E]), reads=[f"xp{i}"], writes=[f"hp{c}"])

        def P1b(t, c):
            j4 = c % 4
            gsl = (8 + c) % 3
            add("dve", lambda e: e.scalar_tensor_tensor(out=sgp[j4][:], in0=sgp[j4][:], scalar=1.0, in1=gsb[gsl][:], op0=ALU.add, op1=ALU.mult),
                reads=[f"gsb{gsl}", f"sgp{j4}"], writes=[f"sgp{j4}"])

        def P2(t, c):
            i = c % 2
            m = c // 2 + 1
            bufs = [xp[i], tmpA, tmpB]
            keys = [f"xp{i}", "tmpA", "tmpB"]
            cur = 0
            lo = 0
            for k in range(1, m + 1):
                sh = 1 << (k - 1)
                nxt = 1 if cur != 1 else 2
                lo2 = lo + sh
                eng = "pool" if k <= POOL_ADDS_ON_POOL else "dve"
                rd = [keys[cur]] + ([f"xp{i}h"] if cur == 0 else [])
                add(eng, lambda e, cur=cur, nxt=nxt, lo2=lo2, sh=sh: e.tensor_tensor(out=bufs[nxt][:, lo2:E], in0=bufs[cur][:, lo2:E], in1=bufs[cur][:, lo2 - sh:E - sh], op=ALU.add),
                    reads=rd, writes=[keys[nxt]])
                cur, lo = nxt, lo2
            w = float(1 << m)
            pk = f"pl{c // 2 % 2}_{c % 2}"
            pdst = pl[c // 2 % 2]
            add("dve", lambda e: e.scalar_tensor_tensor(out=pdst[:, (c % 2) * T:(c % 2 + 1) * T], in0=bufs[cur][:, 16:E], scalar=1.0 / w,
                                                        in1=xp[i][:, 16:E], op0=ALU.mult, op1=ALU.subtract),
                reads=[keys[cur], f"xp{i}"], writes=[pk])
            if t == 0:
                add("dve", lambda e: e.tensor_tensor(out=bufs[cur][:, 16:32], in0=bufs[cur][:, 16:32], in1=invc[:, c * 16:(c + 1) * 16], op=ALU.mult),
                    reads=[keys[cur], "invc", pk], writes=[keys[cur]])
                add("dve", lambda e: e.tensor_tensor(out=pdst[:, (c % 2) * T:(c % 2) * T + 16], in0=bufs[cur][:, 16:32], in1=xp[i][:, 16:32], op=ALU.subtract),
                    reads=[keys[cur], f"xp{i}"], writes=[pk])

        def P3(t, g):
            pi = g % 2

            def mm(e):
                r = []
                for mm_ in range(2):
                    for kk in range(2):
                        r.append(e.matmul(pq[:, mm_ * T:(mm_ + 1) * T], lhsT=pw_bf[:, (g * 2 + kk) * 256 + mm_ * 128:(g * 2 + kk) * 256 + (mm_ + 1) * 128],
                                          rhs=pl[pi][:, kk * T:(kk + 1) * T], start=(kk == 0), stop=(kk == 1)))
                return r
            add("pe", mm, reads=["pw_bf", f"pl{pi}_0", f"pl{pi}_1"], writes=["pq"], banks=["PQ"])

        def P4(t, g):
            for mm_ in range(2):
                c = 2 * g + mm_
                i = c % 4
                add("dve", lambda e, c=c, i=i, mm_=mm_: e.scalar_tensor_tensor(out=yT[:, (8 + c) * T:(9 + c) * T], in0=pq[:, mm_ * T:(mm_ + 1) * T], scalar=vcol(72, c), in1=sgp[i][:],
                                                                             op0=ALU.add, op1=ALU.mult),
                    reads=["pq", "vecs", f"sgp{i}"], writes=[f"yT{8 + c}"], banks=["PQ"])

        def out_sub(src, stag, dst, dtag, t, s):
            g = t * NS + s
            sl = g % 2
            row0 = t * T + s * 128
            add("sp", lambda e: e.dma_start(out=xo[sl][:], in_=src[row0:row0 + 128, :]), reads=[f"dram_{stag}_{g}"], writes=[f"xo{sl}"], dkey=f"xo{sl}")

            def mm(e):
                r = []
                for hh in range(2):
                    for cc in range(16):
                        r.append(e.matmul(po[:, hh * 512:(hh + 1) * 512], lhsT=yT[:, cc * T + s * 128: cc * T + (s + 1) * 128],
                                          rhs=w_out_bf[:, cc * 1024 + hh * 512: cc * 1024 + (hh + 1) * 512], start=(cc == 0), stop=(cc == 15)))
                return r
            add("pe", mm, reads=["w_out_bf"] + [f"yT{cc}" for cc in range(16)], writes=["po"], banks=["PO0", "PO1"])
            add("act", lambda e: e.activation(out=tmpo[:], in_=po[:], func=AF.Square, accum_out=stat[:, 8:9]), reads=["po"], writes=["tmpo", "ssO"], banks=["PO0", "PO1"])
            add("pool", lambda e: e.tensor_scalar(out=stat[:, 9:10], in0=stat[:, 8:9], scalar1=1.0 / D, scalar2=EPS, op0=ALU.mult, op1=ALU.add), reads=["ssO"], writes=["vO"])
            add("pool", lambda e: e.tensor_tensor(out=stat[:, 10:11], in0=stat[:, 9:10], in1=mhalf[:, 0:1], op=ALU.pow), reads=["vO", "mhalf"], writes=["rsO"])
            add("dve", lambda e: e.scalar_tensor_tensor(out=tmpo[:], in0=po[:], scalar=stat[:, 10:11], in1=gg[:], op0=ALU.mult, op1=ALU.mult),
                reads=["po", "rsO", "gg"], writes=["tmpo"], banks=["PO0", "PO1"])
            add("pool", lambda e: e.tensor_tensor(out=xo[sl][:], in0=xo[sl][:], in1=tmpo[:], op=ALU.add), reads=[f"xo{sl}", "tmpo"], writes=[f"xo{sl}"])
            add("sp", lambda e: e.dma_start(out=dst[row0:row0 + 128, :], in_=xo[sl][:]), reads=[f"xo{sl}"], writes=[f"dram_{dtag}_{g}"], dkey=f"xo{sl}")

        NI = 16
        for li, l in enumerate(layers):
            src = x_in if li == 0 else mids[li - 1]
            dst = out_d if li == NL - 1 else mids[li]
            stag = "x" if li == 0 else f"mid{li - 1}"
            dtag = "out" if li == NL - 1 else f"mid{li}"
            prologue(l)
            for s in range(NS):
                stage_a_sub(src, stag, 0, s)
            total = nt * NI

            def item(gs, off):
                n = gs - off
                if 0 <= n < total:
                    t_, j_ = divmod(n, NI)
                    return (t_, j_ < 8, j_ % 8)
                return None

            for gs in range(total + 6):
                it4, it3, it2, it1, it0 = item(gs, 4), item(gs, 3), item(gs, 2), item(gs, 1), item(gs, 0)
                if it4 and not it4[1] and it4[2] % 2 == 1:
                    P4(it4[0], it4[2] // 2)
                if it0:
                    (R0 if it0[1] else P0)(it0[0], it0[2])
                if it3:
                    if it3[1]:
                        R3(it3[0], it3[2])
                    elif it3[2] % 2 == 1:
                        P3(it3[0], it3[2] // 2)
                if it1:
                    (R1a if it1[1] else P1a)(it1[0], it1[2])
                if it2:
                    (R2 if it2[1] else P2)(it2[0], it2[2])
                if it1:
                    (R1b if it1[1] else P1b)(it1[0], it1[2])
                if it4 and it4[1]:
                    R4(it4[0], it4[2])
                if gs < total:
                    t, j = divmod(gs, NI)
                    if t > 0 and j in OUT_HOOK:
                        out_sub(src, stag, dst, dtag, t - 1, OUT_HOOK.index(j))
                    if t + 1 < nt and j in STAGEA_HOOK:
                        stage_a_sub(src, stag, t + 1, STAGEA_HOOK.index(j))
                    if j == 12:
                        for c in range(8):
                            back_act(t, c)
                    if 12 <= j <= 15:
                        back_pair(t, 2 * (j - 12))
            for s in range(NS):
                out_sub(src, stag, dst, dtag, nt - 1, s)
            barrier(U_MAIN_KEYS, "U_main_done")

        S.emit(nc, block, sems, dsems, final_wait_keys=["xo0", "xo1"])
    return nc


def _host_layout(inp):
    f = np.float32
    L = DEPTH

    def pk(v):
        return np.ascontiguousarray(np.asarray(v, f).reshape(L, 8, 128).transpose(0, 2, 1))

    ada_w = np.asarray(inp["ada_w"], f).reshape(L, 2, 4, 128, 6, 512)
    ada_w = np.ascontiguousarray(ada_w.transpose(0, 4, 1, 3, 2, 5)).reshape(L * 6 * 2, 128, 2048)
    w_in = np.asarray(inp["w_in"], f).reshape(L, 2, 4, 128, 8, 512)
    w_in = np.ascontiguousarray(w_in.transpose(0, 4, 1, 3, 2, 5)).reshape(L * 8 * 2, 128, 2048)
    w_out = np.asarray(inp["w_out"], f).reshape(L, 4, 2, 2, 128, 1024)
    w_out = np.ascontiguousarray(w_out.transpose(0, 1, 2, 4, 3, 5)).reshape(L * 4 * 2, 128, 2048)
    ga_w = np.ascontiguousarray(np.asarray(inp["gate_a_w"], f).transpose(0, 2, 1, 3)).reshape(L, 128, 1024)
    gx_w = np.ascontiguousarray(np.asarray(inp["gate_x_w"], f).transpose(0, 2, 1, 3)).reshape(L, 128, 1024)
    pw = np.asarray(inp["pool_w"], f).reshape(L, 4, 2, 128, 256)
    pw = np.ascontiguousarray(pw.transpose(0, 3, 1, 2, 4)).reshape(L, 128, 2048)
    conv_w = np.asarray(inp["conv_w"], f).reshape(L, 4, 8, 128)
    conv_w = np.ascontiguousarray(conv_w.transpose(0, 3, 2, 1)).reshape(L, 128, 32)
    vecs = np.concatenate([pk(inp["pre_norm_g"]), conv_w, pk(inp["conv_b"]),
                           pk(np.asarray(inp["gate_a_b"], f).reshape(L, 1024)), pk(np.asarray(inp["gate_x_b"], f).reshape(L, 1024)),
                           pk(inp["lru_lambda"]), pk(np.asarray(inp["pool_b"], f).reshape(L, 1024)), pk(inp["pool_scale"])], axis=2)
    assert vecs.shape == (L, 128, NV)
    ada_b_bc = np.ascontiguousarray(np.broadcast_to(np.asarray(inp["ada_b"], f)[:, None, :], (L, 128, 3 * D)))
    post_g_bc = np.ascontiguousarray(np.broadcast_to(np.asarray(inp["post_norm_g"], f)[:, None, :], (L, 128, D)))
    invcnt = np.zeros((128, 8, 16), f)
    for c in range(8):
        w = 2 << (c // 2)
        invcnt[:, c, :] = 1.0 / np.minimum(np.arange(16) + 1, w)
    shared = {"ada_w": ada_w, "ada_b_bc": ada_b_bc, "post_g_bc": post_g_bc, "w_in": w_in, "w_out": w_out, "ga_w": ga_w, "gx_w": gx_w,
              "pool_w": pw, "vecs": np.ascontiguousarray(vecs), "ident": np.eye(128, dtype=f), "invcnt": invcnt.reshape(128, 128)}
    c = np.asarray(inp["c"], f)
    cbcs = []
    for b in range(c.shape[0]):
        cb = c[b].reshape(8, 128).T
        cbcs.append(np.ascontiguousarray(np.broadcast_to(cb[:, :, None], (128, 8, 128))).reshape(128, 1024))
    return shared, cbcs


LAUNCH_GROUPS = [[0, 1]]


def kernel(**inputs):
    x = np.asarray(inputs["x"], np.float32)
    nb = x.shape[0]
    shared, cbcs = _host_layout(inputs)
    cur = [np.ascontiguousarray(x[b]) for b in range(nb)]
    for grp in LAUNCH_GROUPS:
        nc = build_nc(list(grp))
        in_maps = [dict(shared, x=cur[b], cbc=cbcs[b]) for b in range(nb)]
        res = run_bass_kernel_spmd(nc, in_maps, core_ids=list(range(nb)))
        cur = [np.asarray(res.results[b]["out"], np.float32) for b in range(nb)]
    return np.stack(cur, axis=0)
```
